# Optimizing a Trainium2 kernel written in Bass

```python
import jax, jax.numpy as jnp
from jax import lax
import numpy as np

D_MODEL = 2048
BATCH = 8
SEQ = 2048
DEPTH = 2

CTX_LEN = 256
GRID_W = 64
NORM_EPS = 1e-6
SSD_WIDTH = D_MODEL // 2
SSD_HEAD_DIM = 64
SSD_HEADS = SSD_WIDTH // SSD_HEAD_DIM
SSD_GROUPS = 2
SSD_STATE = 128
SSD_CONV_W = 5
SSD_CHUNK = 128
SSD_GN = SSD_GROUPS * SSD_STATE
SSD_XBC = SSD_WIDTH + 2 * SSD_GN
SSD_COLS = SSD_WIDTH + SSD_XBC + 2 * SSD_HEADS
SC_WIDTH = D_MODEL - SSD_WIDTH
SC_CONV_W = 3
SC_COLS = 3 * SC_WIDTH
IN_COLS = SSD_COLS + SC_COLS
MIX_WIDTH = SSD_WIDTH + SC_WIDTH
FFN_HIDDEN = ((8 * D_MODEL + 3 * 256 - 1) // (3 * 256)) * 256

kernel_name = "hybrid_ssd_shortconv_prefix_dit"


def rmsnorm(x, g):
    xf = x.astype(jnp.float32)
    y = xf * lax.rsqrt(jnp.mean(xf * xf, axis=-1, keepdims=True) + NORM_EPS)
    return (y * g.astype(jnp.float32)).astype(x.dtype)


def modulate(h, shift, scale):
    return h * (1.0 + scale[:, None, :]) + shift[:, None, :]


def dwconv(u, w):
    k = w.shape[0]
    return lax.conv_general_dilated(u, w.astype(u.dtype)[:, None, :], window_strides=(1,),
                                    padding=[(k // 2, k // 2)],
                                    dimension_numbers=("NWC", "WIO", "NWC"),
                                    feature_group_count=u.shape[-1])


def flip_seq(t):
    return jnp.flip(t, axis=1)


def to_col_major(t, rows):
    b, L, ch = t.shape
    return t.reshape(b, rows, GRID_W, ch).transpose(0, 2, 1, 3).reshape(b, L, ch)


def from_col_major(t, rows):
    b, L, ch = t.shape
    return t.reshape(b, GRID_W, rows, ch).transpose(0, 2, 1, 3).reshape(b, L, ch)


def zero_state(b):
    return jnp.zeros((b, SSD_HEADS, SSD_HEAD_DIM, SSD_STATE), jnp.float32)


def ssd_scan(xs, dt, a, bm, cm, h0, return_y=True):
    b, L, H, P = xs.shape
    G, N = bm.shape[-2], bm.shape[-1]
    R = H // G
    nc = L // SSD_CHUNK
    X = (xs * dt[..., None]).reshape(b, nc, SSD_CHUNK, G, R, P)
    a_cs = jnp.cumsum((dt * a).reshape(b, nc, SSD_CHUNK, G, R), axis=2)
    Bc = bm.reshape(b, nc, SSD_CHUNK, G, N)
    decay_to_end = jnp.exp(a_cs[:, :, -1:] - a_cs)
    states = jnp.einsum("bclgn,bclgr,bclgrp->bcgrpn", Bc, decay_to_end, X)
    chunk_decay = jnp.exp(a_cs[:, :, -1])

    def step(h, inp):
        st, dec = inp
        return dec[..., None, None] * h + st, h

    h_last, h_enter = lax.scan(step, h0.reshape(b, G, R, P, N),
                               (jnp.moveaxis(states, 1, 0), jnp.moveaxis(chunk_decay, 1, 0)))
    h_last = h_last.reshape(b, H, P, N)
    if not return_y:
        return h_last
    h_enter = jnp.moveaxis(h_enter, 0, 1)
    Cc = cm.reshape(b, nc, SSD_CHUNK, G, N)
    lower = jnp.tril(jnp.ones((SSD_CHUNK, SSD_CHUNK), dtype=bool))
    seg = a_cs[:, :, :, None] - a_cs[:, :, None, :]
    decay_ls = jnp.exp(jnp.where(lower[:, :, None, None], seg, -jnp.inf))
    cb = jnp.einsum("bclgn,bcsgn->bclsg", Cc, Bc)
    y_diag = jnp.einsum("bclsgr,bcsgrp->bclgrp", cb[..., None] * decay_ls, X)
    y_off = jnp.einsum("bclgn,bcgrpn,bclgr->bclgrp", Cc, h_enter, jnp.exp(a_cs))
    return (y_diag + y_off).reshape(b, L, H, P), h_last


def ssd_inputs(p_ssd, conv_w, conv_b, dt_bias, a_log):
    b, L, _ = p_ssd.shape
    f32 = jnp.float32
    xbc = jax.nn.silu(dwconv(p_ssd[..., SSD_WIDTH:SSD_WIDTH + SSD_XBC], conv_w) + conv_b.astype(p_ssd.dtype))
    xs = xbc[..., :SSD_WIDTH].astype(f32).reshape(b, L, SSD_HEADS, SSD_HEAD_DIM)
    bm = xbc[..., SSD_WIDTH:SSD_WIDTH + SSD_GN].astype(f32).reshape(b, L, SSD_GROUPS, SSD_STATE)
    cm = xbc[..., SSD_WIDTH + SSD_GN:].astype(f32).reshape(b, L, SSD_GROUPS, SSD_STATE)
    dt_raw = p_ssd[..., SSD_WIDTH + SSD_XBC:SSD_COLS].astype(f32).reshape(b, L, 2, SSD_HEADS)
    dt = jax.nn.softplus(dt_raw + dt_bias.astype(f32))
    a = -jnp.exp(a_log.astype(f32))
    return xs, bm, cm, dt, a


def bidir_ssd(xs, bm, cm, dt, a, h0_f, h0_b, return_y=True):
    fwd = ssd_scan(xs, dt[:, :, 0], a[0], bm, cm, h0_f, return_y)
    bwd = ssd_scan(flip_seq(xs), flip_seq(dt[:, :, 1]), a[1], flip_seq(bm), flip_seq(cm), h0_b, return_y)
    if not return_y:
        return fwd, bwd
    (y_f, h_f), (y_b, h_b) = fwd, bwd
    return y_f + flip_seq(y_b), h_f, h_b


def token_mixers(proj, conv_w, conv_b, dt_bias, a_log, d_skip, ssd_g, sc_w, h0_f, h0_b):
    b, L, _ = proj.shape
    z = proj[..., :SSD_WIDTH]
    xs, bm, cm, dt, a = ssd_inputs(proj, conv_w, conv_b, dt_bias, a_log)
    y, h_f, h_b = bidir_ssd(xs, bm, cm, dt, a, h0_f, h0_b)
    y = (y + d_skip.astype(jnp.float32)[:, None] * xs).reshape(b, L, SSD_WIDTH).astype(proj.dtype)
    y_ssd = rmsnorm(y * jax.nn.silu(z), ssd_g)
    gate_b, gate_c, val = jnp.split(proj[..., SSD_COLS:], 3, axis=-1)
    y_sc = gate_b * dwconv(gate_c * val, sc_w)
    return jnp.concatenate([y_ssd, y_sc], axis=-1), h_f, h_b


def swiglu(h, w_gate, w_up, w_down):
    return (jax.nn.silu(h @ w_gate) * (h @ w_up)) @ w_down


def setup_inputs(seed: int = 0) -> dict:
    key = jax.random.key(seed)
    ks = jax.random.split(key, 24)
    f32 = jnp.float32
    nrm = lambda k, shape, s: jax.random.normal(k, shape, f32) * s
    dt0 = jnp.exp(jax.random.uniform(ks[10], (DEPTH, 2, SSD_HEADS), f32, np.log(1e-3), np.log(1e-1)))
    return {
        "x": nrm(ks[0], (BATCH, SEQ, D_MODEL), 1.0),
        "c": nrm(ks[1], (BATCH, D_MODEL), 1.0),
        "ctx": nrm(ks[2], (BATCH, CTX_LEN, D_MODEL), 1.0),
        "c_ctx": nrm(ks[3], (D_MODEL,), 1.0),
        "ada_w": nrm(ks[4], (DEPTH, D_MODEL, 6 * D_MODEL), 0.5 * D_MODEL ** -0.5),
        "ada_b": nrm(ks[5], (DEPTH, 6 * D_MODEL), 0.01),
        "mix_norm_g": 1.0 + nrm(ks[6], (DEPTH, D_MODEL), 0.05),
        "w_in": nrm(ks[7], (DEPTH, D_MODEL, IN_COLS), D_MODEL ** -0.5),
        "ssd_conv_w": nrm(ks[8], (DEPTH, SSD_CONV_W, SSD_XBC), SSD_CONV_W ** -0.5),
        "ssd_conv_b": nrm(ks[9], (DEPTH, SSD_XBC), 0.01),
        "ssd_dt_bias": dt0 + jnp.log(-jnp.expm1(-dt0)),
        "ssd_a_log": jnp.log(jax.random.uniform(ks[11], (DEPTH, 2, SSD_HEADS), f32, 1.0, 16.0)),
        "ssd_d": 1.0 + nrm(ks[12], (DEPTH, SSD_HEADS), 0.05),
        "ssd_norm_g": 1.0 + nrm(ks[13], (DEPTH, SSD_WIDTH), 0.05),
        "sc_conv_w": nrm(ks[14], (DEPTH, SC_CONV_W, SC_WIDTH), SC_CONV_W ** -0.5),
        "w_out": nrm(ks[15], (DEPTH, MIX_WIDTH, D_MODEL), MIX_WIDTH ** -0.5),
        "ffn_norm_g": 1.0 + nrm(ks[16], (DEPTH, D_MODEL), 0.05),
        "w_gate": nrm(ks[17], (DEPTH, D_MODEL, FFN_HIDDEN), D_MODEL ** -0.5),
        "w_up": nrm(ks[18], (DEPTH, D_MODEL, FFN_HIDDEN), D_MODEL ** -0.5),
        "w_down": nrm(ks[19], (DEPTH, FFN_HIDDEN, D_MODEL), FFN_HIDDEN ** -0.5),
        "final_norm_g": 1.0 + nrm(ks[20], (D_MODEL,), 0.05),
    }


def reference(x, c, ctx, c_ctx, ada_w, ada_b, mix_norm_g, w_in, ssd_conv_w, ssd_conv_b, ssd_dt_bias,
              ssd_a_log, ssd_d, ssd_norm_g, sc_conv_w, w_out, ffn_norm_g, w_gate, w_up, w_down,
              final_norm_g):
    rows = x.shape[1] // GRID_W
    h_ctx = ctx
    for l in range(DEPTH):
        last = l == DEPTH - 1
        mx = jnp.split(jax.nn.silu(c) @ ada_w[l] + ada_b[l], 6, axis=-1)
        mc = jnp.split(jax.nn.silu(c_ctx)[None] @ ada_w[l] + ada_b[l], 6, axis=-1)
        layer_mix = (ssd_conv_w[l], ssd_conv_b[l], ssd_dt_bias[l], ssd_a_log[l], ssd_d[l],
                     ssd_norm_g[l], sc_conv_w[l])

        hc = modulate(rmsnorm(h_ctx, mix_norm_g[l]), mc[0], mc[1])
        if last:
            proj_c = hc @ w_in[l][:, :SSD_COLS]
            xs_c, bm_c, cm_c, dt_c, a_c = ssd_inputs(proj_c, ssd_conv_w[l], ssd_conv_b[l],
                                                     ssd_dt_bias[l], ssd_a_log[l])
            state_f, state_b = bidir_ssd(xs_c, bm_c, cm_c, dt_c, a_c, zero_state(hc.shape[0]),
                                         zero_state(hc.shape[0]), return_y=False)
        else:
            mix_c, state_f, state_b = token_mixers(hc @ w_in[l], *layer_mix,
                                                   zero_state(hc.shape[0]), zero_state(hc.shape[0]))
            h_ctx = h_ctx + mc[2][:, None, :] * (mix_c @ w_out[l])
            hf = modulate(rmsnorm(h_ctx, ffn_norm_g[l]), mc[3], mc[4])
            h_ctx = h_ctx + mc[5][:, None, :] * swiglu(hf, w_gate[l], w_up[l], w_down[l])

        hx = modulate(rmsnorm(x, mix_norm_g[l]), mx[0], mx[1])
        col_major = l % 2 == 1
        if col_major:
            hx = to_col_major(hx, rows)
        mix_x, _, _ = token_mixers(hx @ w_in[l], *layer_mix, state_f, state_b)
        if col_major:
            mix_x = from_col_major(mix_x, rows)
        x = x + mx[2][:, None, :] * (mix_x @ w_out[l])
        hf = modulate(rmsnorm(x, ffn_norm_g[l]), mx[3], mx[4])
        x = x + mx[5][:, None, :] * swiglu(hf, w_gate[l], w_up[l], w_down[l])
    return rmsnorm(x, final_norm_g)
```

```python
import numpy as np
import ml_dtypes
import concourse.bass as bass
import concourse.mybir as mybir
from contextlib import ExitStack
from concourse.bass_utils import run_bass_kernel_spmd

F32 = mybir.dt.float32
BF16 = mybir.dt.bfloat16
AF = mybir.ActivationFunctionType
ALU = mybir.AluOpType
AX = mybir.AxisListType

SAME_ENGINE_SYNC = True


class Tile:
    __slots__ = ("name", "last_w", "rd_eng", "rd_dma", "excl")

    def __init__(self, name):
        self.name = name
        self.excl = name.startswith("ps") or name == "mp_ps"
        self.last_w = None
        self.rd_eng = {}
        self.rd_dma = []


class Op:
    __slots__ = ("eng", "fn", "deps", "is_dma", "needs_inc", "inc_val", "sem_k", "sem_val", "prev_val", "gidx")

    def __init__(self, eng, fn, is_dma):
        self.eng = eng
        self.fn = fn
        self.is_dma = is_dma
        self.deps = []
        self.needs_inc = False
        self.inc_val = 0
        self.sem_k = -1
        self.sem_val = 0
        self.prev_val = 0


class _Rec:
    def __init__(self):
        self.call = None

    def __getattr__(self, name):
        def f(*a, **k):
            self.call = (name, a, k)
            return self
        return f


class Sched:
    ENGS = ("pe", "act", "dve", "pool", "sp")

    def __init__(self, nc, n_dma_sems=40):
        self.nc = nc
        self.ops = {e: [] for e in self.ENGS}
        self.all = []
        self.n_dma_sems = n_dma_sems
        self.tiles = {}
        self._dmas_since_bar = []
        self._bar_pending = {}

    def T(self, name):
        t = self.tiles.get(name)
        if t is None:
            t = Tile(name)
            self.tiles[name] = t
        return t

    def _track(self, o, reads, writes):
        deps = {}

        def add(p, kind):
            if p is None or p is o:
                return
            if p.eng == o.eng and not p.is_dma and not o.is_dma:
                if o.eng == "pe":
                    return
                if kind in ("war", "waw", "rar") or not SAME_ENGINE_SYNC:
                    return
            deps[id(p)] = p

        for t in reads:
            add(t.last_w, "raw")
            if t.excl:
                for r in t.rd_eng.values():
                    add(r, "rar")
        for t in writes:
            add(t.last_w, "waw")
            for r in t.rd_eng.values():
                add(r, "war")
            for r in t.rd_dma:
                add(r, "war")
        for t in reads:
            if o.is_dma:
                t.rd_dma.append(o)
            else:
                t.rd_eng[o.eng] = o
        for t in writes:
            t.last_w = o
            t.rd_eng = {}
            t.rd_dma = []
        if o.eng in self._bar_pending:
            for p in self._bar_pending.pop(o.eng):
                if p is not o and not (p.eng == o.eng and not p.is_dma and not o.is_dma):
                    deps[id(p)] = p
        if o.is_dma:
            self._dmas_since_bar.append(o)
        o.deps = list(deps.values())
        for p in o.deps:
            if not p.is_dma:
                p.needs_inc = True
        self.ops[o.eng].append(o)
        self.all.append(o)

    def op(self, eng, fn, reads=(), writes=()):
        r = _Rec(); fn(r)
        o = Op(eng, r.call, False)
        self._track(o, reads, writes)
        return o

    def dma(self, eng, fn, reads=(), writes=()):
        r = _Rec(); fn(r)
        o = Op(eng, r.call, True)
        self._track(o, reads, writes)
        return o

    def barrier(self):
        B = []
        for e in self.ENGS:
            for o in reversed(self.ops[e]):
                if not o.is_dma:
                    B.append(o)
                    break
        B.extend(self._dmas_since_bar)
        self._dmas_since_bar = []
        self._bar_pending = {e: B for e in self.ENGS}

    def finalize(self, final_wait_eng="sp"):
        nc = self.nc
        with ExitStack() as es:
            prog = {e: es.enter_context(nc.semaphore("pg_" + e)) for e in ("pe", "act", "dve", "pool")}
            dsem = [es.enter_context(nc.semaphore("dm%d" % i)) for i in range(self.n_dma_sems)]
            cnt = {e: 0 for e in self.ENGS}
            uses = [0] * self.n_dma_sems
            k = 0
            for o in self.all:
                if o.is_dma:
                    o.sem_k = k
                    o.prev_val = uses[k] * 16
                    uses[k] += 1
                    o.sem_val = uses[k] * 16
                    k = (k + 1) % self.n_dma_sems
                elif o.needs_inc:
                    cnt[o.eng] += 1
                    o.inc_val = cnt[o.eng]
            self.stats = dict(cnt)
            self.stats.update({"n_" + e: len(self.ops[e]) for e in self.ENGS})
            final_waits = [(dsem[i], uses[i] * 16) for i in range(self.n_dma_sems) if uses[i] > 0]

            def emit_engine(e, name):
                waited = {}

                def wait(sem, val, key):
                    if waited.get(key, 0) >= val:
                        return
                    e.wait_ge(sem, val)
                    waited[key] = val

                for o in self.ops[name]:
                    for p in o.deps:
                        if p.is_dma:
                            wait(dsem[p.sem_k], p.sem_val, ("d", p.sem_k))
                        else:
                            wait(prog[p.eng], p.inc_val, ("p", p.eng))
                    if o.is_dma and o.prev_val > 0:
                        wait(dsem[o.sem_k], o.prev_val, ("d", o.sem_k))
                    ins = getattr(e, o.fn[0])(*o.fn[1], **o.fn[2])
                    if o.is_dma:
                        ins.then_inc(dsem[o.sem_k], 16)
                    elif o.needs_inc:
                        ins.then_inc(prog[name], 1)
                if name == final_wait_eng:
                    for sem, val in final_waits:
                        wait(sem, val, ("f", id(sem)))

            with nc.Block() as block:
                @block.tensor
                def _(e):
                    emit_engine(e, "pe")

                @block.scalar
                def _(e):
                    emit_engine(e, "act")

                @block.vector
                def _(e):
                    emit_engine(e, "dve")

                @block.gpsimd
                def _(e):
                    emit_engine(e, "pool")

                @block.sync
                def _(e):
                    emit_engine(e, "sp")


class Arena:
    def __init__(self, ap_f32, ncols):
        self.ap = ap_f32
        self.ncols = ncols
        self.off = 0
        self.peak = 0

    def mark(self):
        return self.off

    def release(self, m):
        self.off = m

    def alloc(self, cols, dtype=F32):
        if dtype == BF16:
            n32 = (cols + 1) // 2
        else:
            n32 = cols
        n32 = (n32 + 7) // 8 * 8
        a = self.ap[:, self.off:self.off + n32]
        self.off += n32
        self.peak = max(self.peak, self.off)
        assert self.off <= self.ncols, "arena overflow %d > %d" % (self.off, self.ncols)
        if dtype == BF16:
            return a.bitcast(BF16)[:, 0:cols]
        return a[:, 0:cols]


D = 2048
NTT = 18
NV = 232
NEG = -30000.0
EPS = 1e-6
OFF_ADAB, OFF_G1, OFF_G2, OFF_GS, OFF_CW, OFF_CB, OFF_SCW = 0, 96, 112, 128, 136, 196, 208
TGROUPS = [(0, 256), (256, 512), (768, 512), (1280, 512), (1792, 512)]


def pcol(tok):
    return tok + 2 if tok < 256 else tok + 6


PADW = 2312


def build_program(debug=False, stop=None, acols=51200):
    nc = bass.Bass("TRN2", target_bir_lowering=False)

    def din(name, shape, dt=F32):
        return nc.dram_tensor(name, shape, dt, kind="ExternalInput").ap()

    x_in = din("x", [2048, D]); ctx_in = din("ctx", [256, D]); cv_in = din("cv", [128, 32])
    ada_w = din("ada_w", [2, D, 6 * D]); adab_in = din("ada_b", [2, 6 * D])
    w_in = din("w_in", [2, D, 5664]); w_out = din("w_out", [2, D, D])
    w_gate = din("w_gate", [2, D, 5632]); w_up = din("w_up", [2, D, 5632]); w_down = din("w_down", [2, 5632, D])
    vecs_in = din("vecs", [128, 2 * NV]); rowv_in = din("rowv", [1, 160]); fng_in = din("fng", [1, D])
    cst_in = din("cst", [128, 768]); idb_in = din("idb", [128, 128], BF16)
    out = nc.dram_tensor("out", [2048, D], F32, kind="ExternalOutput").ap()
    kind_s = "ExternalOutput" if debug else "Internal"

    def dscr(name, shape, dt):
        return nc.dram_tensor(name, shape, dt, kind=kind_s).ap()

    xres = dscr("xres", [2304, D], F32)
    sz_d = dscr("sz_d", [NTT, 128, 1024], BF16)
    xs_d = dscr("xs_d", [NTT, 128, 1024], BF16)
    bt_d = dscr("bt_d", [NTT, 128, 256], BF16)
    ysc_d = dscr("ysc_d", [8, 128, 2304], BF16)
    acs_d = dscr("acs_d", [NTT, 32, 128], F32)
    hinb_d = dscr("hinb_d", [NTT, 128, 1024], BF16)
    g1_d = dscr("g1_d", [2, 128, D], F32)
    wgs_d = dscr("wgs_d", [22, 128, 16 * 512], BF16)
    wds_d = dscr("wds_d", [8, 128, 44 * 256], BF16)
    dbg = {}
    if debug:
        dbg["hT"] = dscr("hT_d", [128, 16 * 2304], BF16)
        dbg["modp"] = dscr("modp_d", [128, 160], F32)
        dbg["mixT"] = dscr("mixT_d", [128, 8 * 2304], BF16)
        dbg["dts"] = dscr("dts_d", [128, 4 * NTT * 32], F32)
        dbg["bct"] = dscr("bct_d", [128, 2 * 2 * PADW], BF16)

    es = ExitStack()
    big = es.enter_context(nc.sbuf_tensor("arena", [128, acols], F32))
    A = Arena(big, acols)
    PSW = [es.enter_context(nc.psum_tensor("psw%d" % i, [128, 1024], F32)) for i in range(4)]

    def bank(i):
        return PSW[i // 2][:, (i % 2) * 512:(i % 2) * 512 + 512]

    S = Sched(nc)
    T = S.T
    cnt = [0]

    def uid(p):
        cnt[0] += 1
        return "%s_%d" % (p, cnt[0])

    def v3(ap, a):
        return ap.rearrange("p (a b) -> p a b", a=a)

    cst = A.alloc(768); vecs = A.alloc(2 * NV); rowbc = A.alloc(160); cv = A.alloc(32)
    idb = A.alloc(128, BF16)
    sT = A.alloc(32, BF16)
    IDF = cst[:, 0:128]; TRIU = cst[:, 128:256]; TRIL = cst[:, 256:384]
    NEGM = [cst[:, 384:512], cst[:, 512:640]]; ONES = cst[:, 640:768]
    S.dma("sp", lambda e: e.dma_start(out=cst, in_=cst_in), writes=[T("cst")])
    S.dma("sp", lambda e: e.dma_start(out=vecs, in_=vecs_in), writes=[T("vecs")])
    S.dma("sp", lambda e: e.dma_start(out=rowbc, in_=rowv_in[0].partition_broadcast(128)), writes=[T("rowbc")])
    S.dma("sp", lambda e: e.dma_start(out=cv, in_=cv_in), writes=[T("cv")])
    S.dma("sp", lambda e: e.dma_start(out=idb, in_=idb_in), writes=[T("idb")])
    S.op("act", lambda e: e.activation(sT, cv, AF.Silu), reads=[T("cv")], writes=[T("sT")])

    sT3g = v3(sT, 16)
    PBLOCKS = [0, 1, 3, 4, 5]

    def ada_step(L, bi, blk, i, mp, mpn):
        wt, wtile = wload(wsrc(ada_w[L], blk * D + i * 512, 512), 512)
        for j in range(4):
            ch = i * 4 + j
            o = (bi * 16 + ch) * 2
            for kc in range(16):
                S.op("pe", lambda e, j=j, kc=kc, o=o: e.matmul(mp[:, o:o + 2], wt[:, kc, j * 128:(j + 1) * 128], sT3g[:, kc, :], start=(kc == 0), stop=(kc == 15)),
                     reads=[wtile, T("sT")], writes=[T(mpn)])

    ada_pending = [(bi, blk, i) for bi, blk in enumerate(PBLOCKS) for i in range(4)]
    ada_ctr = [0]

    NW = 4
    wring = []
    wstate = {"i": 0}

    def wload(src3, width, kchunks=16):
        nonlocal_ring = None
        i = wstate["i"]; wstate["i"] += 1
        nw = len(wring)
        buf = wring[i % nw]
        t = T("wring%d" % (i % nw))
        dst = buf[:, 0:kchunks * width].rearrange("p (k n) -> p k n", k=kchunks)
        S.dma("pool", lambda e: e.dma_start(out=dst, in_=src3), writes=[t])
        return dst, t

    def wreload(tile_id, width, kchunks=16):
        i = wstate["i"]; wstate["i"] += 1
        nw = len(wring)
        buf = wring[i % nw]
        t = T("wring%d" % (i % nw))
        flat = buf[:, 0:kchunks * width]
        S.dma("sp", lambda e: e.dma_start(out=flat, in_=wgs_d[tile_id]), reads=[T("wgsd%d" % tile_id)], writes=[t])
        return flat.rearrange("p (k n) -> p k n", k=kchunks), t

    def wsave(ap3, t, tile_id, width, kchunks=16):
        flat = ap3.rearrange("p k n -> p (k n)")
        S.dma("sp", lambda e: e.dma_start(out=wgs_d[tile_id], in_=flat), reads=[t], writes=[T("wgsd%d" % tile_id)])

    def wsrc(w2d, c0, width):
        return w2d.rearrange("(k p) n -> p k n", p=128)[:, :, c0:c0 + width]

    def tile_rows(layer, tt, first_src):
        if tt < 2:
            src = ctx_in if first_src else xres
            return [(src[tt * 128:(tt + 1) * 128, :], 0, 128)], ["xr%d" % tt]
        j0 = (tt - 2) * 128
        if layer == 0:
            if first_src:
                return [(x_in[j0:j0 + 128, :], 0, 128)], ["xr%d" % tt]
            return [(xres[256 + j0:256 + j0 + 128, :], 0, 128)], ["xr%d" % tt]
        lat = xres[256:2304, :].rearrange("(r w) f -> w r f", w=64)
        w0 = j0 // 32
        return [(lat[w0 + i], 32 * i, 32) for i in range(4)], ["xr%d" % t for t in range(2, NTT)]

    def nat_rows(tt):
        return xres[tt * 128:(tt + 1) * 128, :]

    def rms_rstd(eng_name, ss, rstd, n, ssname, rname):
        S.op("dve", lambda e: e.tensor_scalar(rstd, ss, 1.0 / n, EPS, ALU.mult, ALU.add), reads=[T(ssname)], writes=[T(rname)])
        S.op("act", lambda e: e.activation(rstd, rstd, AF.Ln), reads=[T(rname)], writes=[T(rname)])
        S.op("act", lambda e: e.activation(rstd, rstd, AF.Exp, scale=-0.5), reads=[T(rname)], writes=[T(rname)])

    def norm_A(xt, xt_name, junk, xnb, xnname, ssb):
        k = uid("n")
        ss = ssb[:, 0:1]; rstd = ssb[:, 1:2]
        S.op("act", lambda e: e.activation(junk, xt, AF.Square, accum_out=ss), reads=[T(xt_name)], writes=[T(k + "ss")])
        rms_rstd("dve", ss, rstd, float(D), k + "ss", k + "rs")
        S.op("act", lambda e: e.activation(xnb, xt, AF.Copy, scale=rstd), reads=[T(xt_name), T(k + "rs")], writes=[T(xnname)])

    def norm_B(xnb, xnname, Aap, Bap, hT3, col0, hname, psb):
        for half in range(2):
            pb = psb[half]
            pbn = "psb%d" % pb
            Pb = bank(pb).bitcast(BF16)
            for j in range(8):
                kc = half * 8 + j
                S.op("pe", lambda e, kc=kc, j=j: e.transpose(Pb[:, j * 128:(j + 1) * 128], xnb[:, kc * 128:(kc + 1) * 128], idb),
                     reads=[T(xnname), T("idb")], writes=[T(pbn)])
            for j in range(8):
                kc = half * 8 + j
                dst = hT3[:, kc, col0:col0 + 128]
                if True:
                    S.op("dve", lambda e, kc=kc, j=j, dst=dst: e.tensor_scalar(dst, Pb[:, j * 128:(j + 1) * 128], Aap[:, kc:kc + 1], Bap[:, kc:kc + 1], ALU.mult, ALU.add),
                         reads=[T(pbn), T("modp")], writes=[T(hname + "k%d" % kc)])
                else:
                    S.op("act", lambda e, kc=kc, j=j, dst=dst: e.activation(dst, Pb[:, j * 128:(j + 1) * 128], AF.Identity, bias=Bap[:, kc:kc + 1], scale=Aap[:, kc:kc + 1]),
                         reads=[T(pbn), T("modp")], writes=[T(hname + "k%d" % kc)])

    def load_tile(rows, toks, xt, name):
        for (src, p0, n) in rows:
            S.dma("sp", lambda e, src=src, p0=p0, n=n: e.dma_start(out=xt[p0:p0 + n, :], in_=src), reads=[T(t) for t in toks], writes=[T(name)])

    def store_tile(rows, toks, xt, name):
        for (dst, p0, n) in rows:
            S.dma("sp", lambda e, dst=dst, p0=p0, n=n: e.dma_start(out=dst, in_=xt[p0:p0 + n, :]), reads=[T(name)], writes=[T(t) for t in toks])

    conv_pending = []

    def conv_fill(L):
        for j2 in range(11):
            conv_pending.append(("g", L, j2)); conv_pending.append(("u", L, j2))
        for n8 in range(8):
            for kq in range(4):
                conv_pending.append(("d", L, n8, kq))

    def conv_step(n=1):
        for _ in range(n):
            if not conv_pending:
                return
            it = conv_pending.pop(0)
            if it[0] in ("g", "u"):
                _, L, j2 = it
                w2 = w_gate[L] if it[0] == "g" else w_up[L]
                tid = j2 if it[0] == "g" else 11 + j2
                S.dma("pool", lambda e: e.dma_start(out=wgs_d[tid].rearrange("p (k n) -> p k n", k=16), in_=wsrc(w2, j2 * 512, 512)),
                      writes=[T("wgsd%d" % tid), T("convchain")])
            else:
                _, L, n8, kq = it
                src = w_down[L].rearrange("(k p) n -> p k n", p=128)[:, kq * 11:(kq + 1) * 11, n8 * 256:(n8 + 1) * 256]
                dst = wds_d[n8].rearrange("p (k n) -> p k n", k=44)[:, kq * 11:(kq + 1) * 11, :]
                S.dma("pool", lambda e: e.dma_start(out=dst, in_=src), writes=[T("wdsd%d_%d" % (n8, kq)), T("convchain")])

    for layer in range(2):
        last = layer == 1
        vo = layer * NV
        m_layer = A.mark()
        modp = A.alloc(160)
        AB = A.alloc(8 * 16)
        G2 = A.alloc(32)

        def ABs(i):
            return AB[:, i * 16:(i + 1) * 16]

        S.barrier()
        mA = A.mark()
        wring = [A.alloc(16 * 512, BF16) for _ in range(NW)]
        wstate["i"] = 0
        sTbc = [A.alloc(16 * 128, BF16), A.alloc(16 * 128, BF16)]
        adab_bc = A.alloc(D)
        g1st = A.alloc(D)
        sT3 = v3(sT, 16)
        for m in range(2):
            S.op("dve", lambda e, m=m: e.tensor_copy(v3(sTbc[m], 16), sT3[:, :, m:m + 1].broadcast_to([128, 16, 128])),
                 reads=[T("sT")], writes=[T("sTbc")])
        S.dma("sp", lambda e: e.dma_start(out=adab_bc, in_=adab_in[layer, 2 * D:3 * D].partition_broadcast(128)), writes=[T("adab_bc")])
        aw = ada_w[layer]
        pblocks = PBLOCKS
        if layer == 0:
            mp_ps = bank(0); mpn = "mp_ps"
            for bi, blk in enumerate(pblocks):
                for i in range(4):
                    ada_step(0, bi, blk, i, mp_ps, mpn)
        else:
            mp_ps = bank(4); mpn = "ps_mp1"
            while ada_pending:
                bi, blk, i = ada_pending.pop(0)
                ada_step(1, bi, blk, i, mp_ps, mpn)
        for bi, blk in enumerate(pblocks):
            S.op("dve", lambda e, bi=bi, blk=blk: e.tensor_tensor(
                v3(modp[:, bi * 32:(bi + 1) * 32], 16), v3(mp_ps[:, bi * 32:(bi + 1) * 32], 16),
                vecs[:, vo + OFF_ADAB + blk * 16: vo + OFF_ADAB + (blk + 1) * 16].unsqueeze(2).broadcast_to([128, 16, 2]), ALU.add),
                reads=[T(mpn), T("vecs")], writes=[T("modp")])
        mp3 = modp.rearrange("p (b k m) -> p b k m", b=5, k=16)
        for m in range(2):
            for n_i, (gofs, bsh, bsc) in enumerate([(OFF_G1, 0, 1), (OFF_G2, 2, 3)]):
                Aap = ABs(n_i * 4 + m * 2); Bap = ABs(n_i * 4 + m * 2 + 1)
                gv = vecs[:, vo + gofs: vo + gofs + 16]
                S.op("dve", lambda e, Aap=Aap, gv=gv, bsc=bsc, m=m: e.scalar_tensor_tensor(Aap, mp3[:, bsc, :, m], 1.0, gv, ALU.add, ALU.mult),
                     reads=[T("modp"), T("vecs")], writes=[T("modp")])
                S.op("dve", lambda e, Bap=Bap, bsh=bsh, m=m: e.tensor_copy(Bap, mp3[:, bsh, :, m]), reads=[T("modp")], writes=[T("modp")])
            S.op("dve", lambda e, m=m: e.tensor_copy(G2[:, m * 16:(m + 1) * 16], mp3[:, 4, :, m]), reads=[T("modp")], writes=[T("modp")])
        if debug and layer == 0:
            S.dma("sp", lambda e: e.dma_start(out=dbg["modp"], in_=modp), reads=[T("modp")], writes=[T("dbg_modp")])
        for i in range(4):
            wt, wtile = wload(wsrc(aw, 2 * D + i * 512, 512), 512)
            for m in range(2):
                if m == 1 and last:
                    continue
                pb = 2 + (i * 2 + m) % 2
                st3 = v3(sTbc[m], 16)
                for kc in range(16):
                    S.op("pe", lambda e, wt=wt, kc=kc, pb=pb, st3=st3: e.matmul(bank(pb), st3[:, kc, :], wt[:, kc, :], start=(kc == 0), stop=(kc == 15)),
                         reads=[wtile, T("sTbc")], writes=[T("psb%d" % pb)])
                gname = "g1st%d" % m
                S.op("dve", lambda e, pb=pb, i=i: e.tensor_tensor(g1st[:, 0:512], bank(pb), adab_bc[:, i * 512:(i + 1) * 512], ALU.add),
                     reads=[T("psb%d" % pb), T("adab_bc")], writes=[T("g1st")])
                S.dma("sp", lambda e, m=m, i=i: e.dma_start(out=g1_d[m, :, i * 512:(i + 1) * 512], in_=g1st[:, 0:512]), reads=[T("g1st")], writes=[T("g1d%d" % m)])
        A.release(mA)
        if stop == "A%d" % layer:
            break

        S.barrier()
        m_mix = A.mark()
        dts = A.alloc(4 * NTT * 32)
        biasall = dts[:, 0:576]; Eall = dts[:, 576:1152]; wstall = dts[:, 1152:1728]; cdall = dts[:, 1728:2304]
        a_bc = A.alloc(32)
        BCT = A.alloc(4 * PADW, BF16)
        BT = [BCT[:, g * PADW:(g + 1) * PADW] for g in range(2)]
        CT = [BCT[:, (2 + g) * PADW:(3 + g) * PADW] for g in range(2)]
        S.op("act", lambda e: e.activation(a_bc, rowbc[:, layer * 80 + 32: layer * 80 + 64], AF.Exp), reads=[T("rowbc")], writes=[T("a_bc")])
        S.op("dve", lambda e: e.tensor_scalar(a_bc, a_bc, -1.0, None, ALU.mult), reads=[T("a_bc")], writes=[T("a_bc")])
        dtb_bc = rowbc[:, layer * 80: layer * 80 + 32]
        dsk_bc = rowbc[:, layer * 80 + 64: layer * 80 + 80]

        mP = A.mark()
        hT = A.alloc(16 * 2304, BF16)
        hT3 = v3(hT, 16)
        mN = A.mark()
        xts = [A.alloc(D), A.alloc(D)]
        junk = A.alloc(D, BF16); xnbs = [A.alloc(D, BF16), A.alloc(D, BF16)]; ssbs = [A.alloc(8), A.alloc(8)]
        tts = list(range(NTT))

        def n1_load(tt):
            rows, toks = tile_rows(layer, tt, layer == 0)
            load_tile(rows, toks, xts[tt % 2], "xt%d" % (tt % 2))

        def n1_A(tt):
            norm_A(xts[tt % 2], "xt%d" % (tt % 2), junk, xnbs[tt % 2], "xnb%d" % (tt % 2), ssbs[tt % 2])

        def n1_B(tt):
            isctx = tt < 2
            norm_B(xnbs[tt % 2], "xnb%d" % (tt % 2), ABs(2 if isctx else 0), ABs(3 if isctx else 1), hT3, tt * 128, "hT", (0, 1))

        n1_load(0); n1_load(1); n1_A(0)
        for tt in tts:
            if tt + 1 < NTT:
                n1_A(tt + 1)
            if tt + 2 < NTT:
                n1_load(tt + 2)
            n1_B(tt)
        A.release(mN)
        if debug and layer == 0:
            S.dma("sp", lambda e: e.dma_start(out=dbg["hT"], in_=hT), reads=[T("hTk%d" % k_) for k_ in range(16)], writes=[T("dbg_hT")])
        if stop == "N%d" % layer:
            break
        S.barrier()
        win = w_in[layer]
        conv_fill(layer)
        mz = A.mark()
        NW = 3
        wring = [A.alloc(16 * 512, BF16) for _ in range(NW)]
        wstate["i"] = 0
        szst = [A.alloc(512, BF16) for _ in range(2)]
        q = 0
        for i in range(2):
            wt, wtile = wload(wsrc(win, i * 512, 512), 512)
            for tt in tts:
                if last and tt < 2:
                    continue
                pb = q % 4; q += 1
                for kc in range(16):
                    S.op("pe", lambda e, kc=kc, tt=tt, wt=wt, pb=pb: e.matmul(bank(pb), hT3[:, kc, tt * 128:(tt + 1) * 128], wt[:, kc, :], start=(kc == 0), stop=(kc == 15)),
                         reads=[T("hTk%d" % kc), wtile], writes=[T("psb%d" % pb)])
                sb = szst[q % 2]; sbn = "szst%d" % (q % 2)
                S.op("act", lambda e, sb=sb, pb=pb: e.activation(sb, bank(pb), AF.Silu), reads=[T("psb%d" % pb)], writes=[T(sbn)])
                S.dma("act", lambda e, sb=sb, tt=tt, i=i: e.dma_start(out=sz_d[tt, :, i * 512:(i + 1) * 512], in_=sb), reads=[T(sbn)], writes=[T("szd%d" % tt)])
        A.release(mz)
        S.barrier()
        NW = 4
        mring = A.mark()
        wring = [A.alloc(16 * 256, BF16) for _ in range(NW)]
        wstate["i"] = 0
        mdt = A.alloc
        mdtm = A.mark()
        wdt = A.alloc(16 * 32, BF16)
        wdt3 = v3(wdt, 16)
        S.dma("pool", lambda e: e.dma_start(out=wdt3, in_=wsrc(win, 2560, 32)), writes=[T("wdt")])
        dtw = [A.alloc(32 * 8) for _ in range(2)]
        acsT_sb = [A.alloc(128) for _ in range(2)]
        for tt in tts:
            w8 = dtw[tt % 2]; wn = "dtw%d" % (tt % 2)
            x1 = w8[:, 0:32]; ex = w8[:, 32:64]; dt_ = w8[:, 64:96]; lndt = w8[:, 96:128]; dA = w8[:, 128:160]; acs = w8[:, 160:192]; dd = w8[:, 192:224]
            pb = 4 + tt % 2; pbn = "psb%d" % pb
            P = bank(pb)
            for kc in range(16):
                S.op("pe", lambda e, kc=kc, tt=tt, P=P: e.matmul(P[:, 0:32], hT3[:, kc, tt * 128:(tt + 1) * 128], wdt3[:, kc, :], start=(kc == 0), stop=(kc == 15)),
                     reads=[T("hTk%d" % kc), T("wdt")], writes=[T(pbn)])
            S.op("dve", lambda e, P=P, x1=x1: e.tensor_tensor(x1, P[:, 0:32], dtb_bc, ALU.add), reads=[T(pbn), T("rowbc")], writes=[T(wn)])
            S.op("act", lambda e, x1=x1, ex=ex: e.activation(ex, x1, AF.Exp), reads=[T(wn)], writes=[T(wn)])
            S.op("act", lambda e, dt_=dt_, ex=ex: e.activation(dt_, ex, AF.Ln, bias=1.0), reads=[T(wn)], writes=[T(wn)])
            S.op("act", lambda e, dt_=dt_, lndt=lndt: e.activation(lndt, dt_, AF.Ln), reads=[T(wn)], writes=[T(wn)])
            S.op("dve", lambda e, dt_=dt_, dA=dA: e.tensor_tensor(dA, dt_, a_bc, ALU.mult), reads=[T(wn), T("a_bc")], writes=[T(wn)])
            S.op("pe", lambda e, P=P, dA=dA: e.matmul(P[:, 32:48], TRIU, dA[:, 0:16], start=True, stop=True), reads=[T(wn), T("cst")], writes=[T(pbn)])
            S.op("pe", lambda e, P=P, dA=dA: e.matmul(P[:, 48:64], TRIL, dA[:, 16:32], start=True, stop=True), reads=[T(wn), T("cst")], writes=[T(pbn)])
            S.op("pe", lambda e, P=P, dA=dA: e.matmul(P[:, 64:96], ONES, dA, start=True, stop=True), reads=[T(wn), T("cst")], writes=[T(pbn)])
            S.op("dve", lambda e, P=P, acs=acs: e.tensor_copy(acs, P[:, 32:64]), reads=[T(pbn)], writes=[T(wn)])
            sl = slice(tt * 32, (tt + 1) * 32)
            S.op("dve", lambda e, lndt=lndt, acs=acs, sl=sl: e.tensor_tensor(biasall[:, sl], lndt, acs, ALU.subtract), reads=[T(wn)], writes=[T("dts")])
            S.op("act", lambda e, acs=acs, sl=sl: e.activation(Eall[:, sl], acs, AF.Exp), reads=[T(wn)], writes=[T("dts")])
            S.op("dve", lambda e, P=P, acs=acs, dd=dd: e.tensor_tensor(dd, P[:, 64:96], acs, ALU.subtract), reads=[T(pbn), T(wn)], writes=[T(wn)])
            S.op("act", lambda e, dd=dd: e.activation(dd, dd, AF.Exp), reads=[T(wn)], writes=[T(wn)])
            S.op("dve", lambda e, dd=dd, dt_=dt_, sl=sl: e.tensor_tensor(wstall[:, sl], dd, dt_, ALU.mult), reads=[T(wn)], writes=[T("dts")])
            S.op("act", lambda e, P=P, sl=sl: e.activation(cdall[:, sl], P[:, 64:96], AF.Exp), reads=[T(pbn)], writes=[T("dts")])
            S.op("pe", lambda e, P=P, acs=acs: e.transpose(P[0:32, 128:256], acs, IDF), reads=[T(wn), T("cst")], writes=[T(pbn)])
            at = acsT_sb[tt % 2]; atn = "acsT%d" % (tt % 2)
            S.op("act", lambda e, P=P, at=at: e.copy(at[0:32, :], P[0:32, 128:256]), reads=[T(pbn)], writes=[T(atn)])
            S.dma("act", lambda e, at=at, tt=tt: e.dma_start(out=acs_d[tt], in_=at[0:32, :]), reads=[T(atn)], writes=[T("acsd%d" % tt)])
        if debug and layer == 0:
            S.dma("sp", lambda e: e.dma_start(out=dbg["dts"], in_=dts), reads=[T("dts")], writes=[T("dbg_dts")])
        A.release(mdtm)
        mx = A.mark()
        raw = A.alloc(PADW); acc = A.alloc(PADW)
        xbcT = [A.alloc(PADW, BF16) for _ in range(2)]
        stg = A.alloc(NTT * 512, BF16)
        stg3 = v3(stg, NTT)
        S.op("dve", lambda e: e.memset(raw, 0.0), writes=[T("raw")])
        cwv = vecs[:, vo + OFF_CW: vo + OFF_CW + 60]
        cbv = vecs[:, vo + OFF_CB: vo + OFF_CB + 12]
        q = 0
        NCV = 2308
        for i in range(6):
            wt, wtile = wload(wsrc(win, 1024 + i * 256, 256), 256)
            conv_step(1)
            for j in range(2):
                blk = i * 2 + j
                for (t0, n) in TGROUPS:
                    pb = q % 4; q += 1
                    for kc in range(16):
                        S.op("pe", lambda e, kc=kc, wt=wt, j=j, pb=pb, t0=t0, n=n: e.matmul(bank(pb)[:, 0:n], wt[:, kc, j * 128:(j + 1) * 128], hT3[:, kc, t0:t0 + n], start=(kc == 0), stop=(kc == 15)),
                             reads=[T("hTk%d" % kc), wtile], writes=[T("psb%d" % pb)])
                    c0 = pcol(t0)
                    S.op("act", lambda e, pb=pb, c0=c0, n=n: e.copy(raw[:, c0:c0 + n], bank(pb)[:, 0:n]), reads=[T("psb%d" % pb)], writes=[T("raw")])
                S.op("dve", lambda e, blk=blk: e.tensor_scalar(acc[:, 2:2 + NCV], raw[:, 0:NCV], cwv[:, blk:blk + 1], None, ALU.mult), reads=[T("raw"), T("vecs")], writes=[T("acc")])
                for k in range(1, 5):
                    S.op("dve", lambda e, blk=blk, k=k: e.scalar_tensor_tensor(acc[:, 2:2 + NCV], raw[:, k:k + NCV], cwv[:, k * 12 + blk:k * 12 + blk + 1], acc[:, 2:2 + NCV], ALU.mult, ALU.add),
                         reads=[T("raw"), T("vecs"), T("acc")], writes=[T("acc")])
                if blk < 10:
                    xb = xbcT[blk % 2]; xbn = "xbcT%d" % (blk % 2)
                elif blk < 12:
                    xb = CT[blk - 10]; xbn = "BCT"
                if blk in (8, 9):
                    xb2 = BT[blk - 8]
                    S.op("act", lambda e, xb2=xb2, blk=blk: e.activation(xb2[:, 2:2 + NCV], acc[:, 2:2 + NCV], AF.Silu, bias=cbv[:, blk:blk + 1]), reads=[T("acc"), T("vecs")], writes=[T("BCT")])
                    xb = xb2; xbn = "BCT"
                else:
                    S.op("act", lambda e, xb=xb, blk=blk: e.activation(xb[:, 2:2 + NCV], acc[:, 2:2 + NCV], AF.Silu, bias=cbv[:, blk:blk + 1]), reads=[T("acc"), T("vecs")], writes=[T(xbn)])
                if blk < 10:
                    jj = blk % 4
                    for tq in range(0, NTT, 4):
                        pb = 4 + (tq // 4) % 2; pbn = "psb%d" % pb
                        Pb = bank(pb).bitcast(BF16)
                        nt = min(4, NTT - tq)
                        for u in range(nt):
                            tt = tq + u
                            c0 = pcol(tt * 128)
                            S.op("pe", lambda e, xb=xb, c0=c0, Pb=Pb, u=u: e.transpose(Pb[:, u * 128:(u + 1) * 128], xb[:, c0:c0 + 128], idb),
                                 reads=[T(xbn), T("idb")], writes=[T(pbn)])
                        S.op("dve", lambda e, Pb=Pb, tq=tq, nt=nt, jj=jj: e.tensor_copy(stg3[:, tq:tq + nt, jj * 128:(jj + 1) * 128], Pb[:, 0:nt * 128].rearrange("p (a b) -> p a b", a=nt)),
                             reads=[T(pbn)], writes=[T("stg")])
                    if blk in (3, 7):
                        h0 = (blk // 4) * 512
                        S.dma("sp", lambda e, h0=h0: e.dma_start(out=xs_d[:, :, h0:h0 + 512].rearrange("t p c -> p t c"), in_=stg3), reads=[T("stg")], writes=[T("xsd")])
                    if blk == 9:
                        S.dma("sp", lambda e: e.dma_start(out=bt_d.rearrange("t p c -> p t c"), in_=stg3[:, :, 0:256]), reads=[T("stg")], writes=[T("btd")])
        if debug and layer == 0:
            S.dma("sp", lambda e: e.dma_start(out=dbg["bct"], in_=BCT), reads=[T("BCT")], writes=[T("dbg_bct")])
        A.release(mx)
        if True:
            msc = A.mark()
            raw = A.alloc(PADW); acc = A.alloc(PADW); ucT = A.alloc(2304)
            yscT = [A.alloc(2304, BF16) for _ in range(2)]
            S.op("dve", lambda e: e.memset(raw, 0.0), writes=[T("raw")])
            scw = vecs[:, vo + OFF_SCW: vo + OFF_SCW + 24]
            q = 0
            for i2 in range(4):
                wgc, tgc = wload(wsrc(win, 3616 + i2 * 256, 256), 256)
                conv_step(1)
                wvl, tvl = wload(wsrc(win, 4640 + i2 * 256, 256), 256)
                conv_step(1)
                wgb, tgb = wload(wsrc(win, 2592 + i2 * 256, 256), 256)
                conv_step(1)
                for j in range(2):
                    blk = i2 * 2 + j
                    for step, (wt, wtile) in enumerate([(wgc, tgc), (wvl, tvl), (wgb, tgb)]):
                        if step == 2:
                            S.op("dve", lambda e, blk=blk: e.tensor_scalar(acc[:, 2:2 + NCV], raw[:, 1:1 + NCV], scw[:, blk:blk + 1], None, ALU.mult), reads=[T("raw"), T("vecs")], writes=[T("acc")])
                            for k in range(1, 3):
                                S.op("dve", lambda e, blk=blk, k=k: e.scalar_tensor_tensor(acc[:, 2:2 + NCV], raw[:, 1 + k:1 + k + NCV], scw[:, k * 8 + blk:k * 8 + blk + 1], acc[:, 2:2 + NCV], ALU.mult, ALU.add),
                                     reads=[T("raw"), T("vecs"), T("acc")], writes=[T("acc")])
                        ys = yscT[blk % 2]; ysn = "yscT%d" % (blk % 2)
                        for (t0, n) in TGROUPS:
                            if last and t0 == 0:
                                continue
                            pb = q % 4; q += 1
                            for kc in range(16):
                                S.op("pe", lambda e, kc=kc, wt=wt, j=j, pb=pb, t0=t0, n=n: e.matmul(bank(pb)[:, 0:n], wt[:, kc, j * 128:(j + 1) * 128], hT3[:, kc, t0:t0 + n], start=(kc == 0), stop=(kc == 15)),
                                     reads=[T("hTk%d" % kc), wtile], writes=[T("psb%d" % pb)])
                            c0 = pcol(t0)
                            if step == 0:
                                S.op("act", lambda e, pb=pb, t0=t0, n=n: e.copy(ucT[:, t0:t0 + n], bank(pb)[:, 0:n]), reads=[T("psb%d" % pb)], writes=[T("ucT")])
                            elif step == 1:
                                S.op("dve", lambda e, pb=pb, t0=t0, n=n, c0=c0: e.tensor_tensor(raw[:, c0:c0 + n], bank(pb)[:, 0:n], ucT[:, t0:t0 + n], ALU.mult),
                                     reads=[T("psb%d" % pb), T("ucT")], writes=[T("raw")])
                            else:
                                S.op("dve", lambda e, pb=pb, t0=t0, n=n, c0=c0, ys=ys: e.tensor_tensor(ys[:, t0:t0 + n], bank(pb)[:, 0:n], acc[:, c0:c0 + n], ALU.mult),
                                     reads=[T("psb%d" % pb), T("acc")], writes=[T(ysn)])
                    lo = 256 if last else 0
                    S.dma("sp", lambda e, ys=ys, blk=blk, lo=lo: e.dma_start(out=ysc_d[blk, :, lo:2304], in_=ys[:, lo:2304]), reads=[T(ysn)], writes=[T("yscd")])
            A.release(msc)
        A.release(mP)
        if stop == "P%d" % layer:
            break

        S.barrier()
        mS = A.mark()
        mixT = A.alloc(8 * 2304, BF16)
        mixT3 = v3(mixT, 8)
        mS2 = A.mark()
        Hs = [A.alloc(1024), A.alloc(1024)]
        hbf = [A.alloc(1024, BF16) for _ in range(2)]
        xs_sb = [A.alloc(1024, BF16) for _ in range(3)]
        bt_sb = [A.alloc(256, BF16) for _ in range(3)]
        szt = [A.alloc(1024, BF16) for _ in range(3)]
        hinb = [A.alloc(1024, BF16) for _ in range(3)]
        Abc = [A.alloc(32 * 128) for _ in range(3)]
        MT = [A.alloc(16 * 128, BF16) for _ in range(2)]
        Xp = A.alloc(1024, BF16)
        t1 = A.alloc(1024); t2 = A.alloc(1024); t3 = A.alloc(1024); ygn = A.alloc(1024); ssb = A.alloc(8)
        gsv = vecs[:, vo + OFF_GS: vo + OFF_GS + 8]

        def h3(ap):
            return ap.rearrange("p (h q) -> p h q", h=16)

        def bc16(ap16):
            return ap16.unsqueeze(2).broadcast_to([128, 16, 64])

        def load_chunk(tt, slot):
            S.dma("sp", lambda e: e.dma_start(out=xs_sb[slot], in_=xs_d[tt]), reads=[T("xsd")], writes=[T("xs_sb%d" % slot)])
            S.dma("sp", lambda e: e.dma_start(out=bt_sb[slot], in_=bt_d[tt]), reads=[T("btd")], writes=[T("bt_sb%d" % slot)])

        def update_H(d, tt, slot):
            Hn = "H%d" % d
            H = Hs[d]
            ws = wstall[:, tt * 32 + d * 16: tt * 32 + d * 16 + 16]
            cd = cdall[:, tt * 32 + d * 16: tt * 32 + d * 16 + 16]
            S.op("dve", lambda e: e.tensor_tensor(h3(Xp), h3(xs_sb[slot]), bc16(ws), ALU.mult), reads=[T("xs_sb%d" % slot), T("dts")], writes=[T("Xp")])
            st = PSW[3]
            for g in range(2):
                S.op("pe", lambda e, g=g: e.matmul(st[:, g * 512:(g + 1) * 512], bt_sb[slot][:, g * 128:(g + 1) * 128], Xp[:, g * 512:(g + 1) * 512], start=True, stop=True),
                     reads=[T("bt_sb%d" % slot), T("Xp")], writes=[T("ps_st")])
            S.op("dve", lambda e: e.tensor_tensor(h3(H), h3(H), bc16(cd), ALU.mult), reads=[T(Hn), T("dts")], writes=[T(Hn)])
            S.op("dve", lambda e: e.tensor_tensor(H, H, st[:, :], ALU.add), reads=[T(Hn), T("ps_st")], writes=[T(Hn)])

        S.op("dve", lambda e: e.memset(Hs[1], 0.0), writes=[T("H1")])
        S.op("dve", lambda e: e.memset(Hs[0], 0.0), writes=[T("H0")])
        orderB = [1, 0] + list(range(NTT - 1, 1, -1))
        for qi, tt in enumerate(orderB):
            slot = qi % 2
            load_chunk(tt, slot)
            conv_step(1)
            if not (last and tt < 2):
                hb = hbf[qi % 2]; hbn = "hbf%d" % (qi % 2)
                S.op("act", lambda e, hb=hb: e.copy(hb, Hs[1]), reads=[T("H1")], writes=[T(hbn)])
                S.dma("act", lambda e, hb=hb, tt=tt: e.dma_start(out=hinb_d[tt], in_=hb), reads=[T(hbn)], writes=[T("hinbd%d" % tt)])
            if tt != 2:
                update_H(1, tt, slot)
        if stop == "SB%d" % layer:
            break
        orderF = list(range(NTT))
        MT4 = [[MT[0], MT[1]], [A.alloc(16 * 128, BF16), A.alloc(16 * 128, BF16)]]
        Xp2 = [Xp, A.alloc(1024, BF16)]
        ygn2 = [ygn, A.alloc(1024)]

        def do_y(tt):
            return not (last and tt < 2)

        def stage_loads(qi):
            tt = orderF[qi]; slot = qi % 2; s3 = qi % 3
            load_chunk(tt, s3)
            if do_y(tt):
                S.dma("sp", lambda e: e.dma_start(out=szt[s3], in_=sz_d[tt]), reads=[T("szd%d" % tt)], writes=[T("szt%d" % s3)])
                S.dma("sp", lambda e: e.dma_start(out=hinb[s3], in_=hinb_d[tt]), reads=[T("hinbd%d" % tt)], writes=[T("hinb%d" % s3)])
                S.dma("sp", lambda e: e.dma_start(out=Abc[s3], in_=acs_d[tt].rearrange("a b -> (a b)").partition_broadcast(128)), reads=[T("acsd%d" % tt)], writes=[T("Abc%d" % s3), T("Abc%d_0" % s3), T("Abc%d_1" % s3)])

        def stage1(qi):
            tt = orderF[qi]; slot = qi % 2; s3 = qi % 3
            c0 = pcol(tt * 128)
            if tt != NTT - 1:
                ws = wstall[:, tt * 32: tt * 32 + 16]
                S.op("dve", lambda e: e.tensor_tensor(h3(Xp2[slot]), h3(xs_sb[s3]), bc16(ws), ALU.mult), reads=[T("xs_sb%d" % s3), T("dts")], writes=[T("Xp%d" % slot)])
            if not do_y(tt):
                return
            cbp = bank(0)[:, slot * 256:(slot + 1) * 256]
            cbn = "ps_cb"
            for g in range(2):
                S.op("pe", lambda e, g=g: e.matmul(cbp[:, g * 128:(g + 1) * 128], BT[g][:, c0:c0 + 128], CT[g][:, c0:c0 + 128], start=True, stop=True),
                     reads=[T("BCT")], writes=[T(cbn)])
            A3 = Abc[s3].rearrange("p (a l) -> p a l", a=32)
            An = "Abc%d" % s3
            for d in range(2):
                Ad = A3[:, d * 16:(d + 1) * 16, :]
                And = An + "_%d" % d
                eng = "pool" if d == 0 else "dve"
                S.op(eng, lambda e, Ad=Ad, d=d: e.tensor_tensor(Ad, Ad, NEGM[d].unsqueeze(1).broadcast_to([128, 16, 128]), ALU.add), reads=[T(An), T("cst")], writes=[T(And)])
                bs = biasall[:, tt * 32 + d * 16: tt * 32 + d * 16 + 16]
                S.op(eng, lambda e, Ad=Ad, bs=bs: e.tensor_tensor(Ad, Ad, bs.unsqueeze(2).broadcast_to([128, 16, 128]), ALU.add), reads=[T(And), T("dts")], writes=[T(And)])
                Me = Ad
                S.op("act", lambda e, Ad=Ad, Me=Me: e.activation(Me, Ad, AF.Exp), reads=[T(And)], writes=[T(And)])
                M3 = v3(MT4[slot][d], 16)
                for g in range(2):
                    S.op("dve", lambda e, g=g, M3=M3, Me=Me: e.tensor_tensor(M3[:, g * 8:(g + 1) * 8, :], Me[:, g * 8:(g + 1) * 8, :],
                                                                        cbp[:, g * 128:(g + 1) * 128].unsqueeze(1).broadcast_to([128, 8, 128]), ALU.mult),
                         reads=[T(And), T(cbn)], writes=[T("MT%d_%d" % (slot, d))])

        def stage3b(qi):
            tt = orderF[qi]; slot = qi % 2; s3 = qi % 3
            yg = ygn2[slot]
            for half in range(2):
                pT = bank(1)
                for u in range(4):
                    kc = half * 4 + u
                    S.op("pe", lambda e, kc=kc, u=u: e.transpose(pT[:, u * 128:(u + 1) * 128], yg[:, kc * 128:(kc + 1) * 128], IDF), reads=[T("ygn%d" % slot), T("cst")], writes=[T("ps_yT")])
                for u in range(4):
                    kc = half * 4 + u
                    S.op("act", lambda e, kc=kc, u=u: e.activation(mixT3[:, kc, tt * 128:(tt + 1) * 128], pT[:, u * 128:(u + 1) * 128], AF.Copy, scale=gsv[:, kc:kc + 1]),
                         reads=[T("ps_yT"), T("vecs")], writes=[T("mixT")])

        def stage2(qi, prev_y):
            tt = orderF[qi]; slot = qi % 2; s3 = qi % 3
            c0 = pcol(tt * 128)
            yps = PSW[1]; yo = PSW[2]; st = PSW[3]
            if do_y(tt):
                for h in range(16):
                    for d in range(2):
                        S.op("pe", lambda e, h=h, d=d: e.matmul(yps[:, h * 64:(h + 1) * 64], v3(MT4[slot][d], 16)[:, h, :], xs_sb[s3][:, h * 64:(h + 1) * 64], start=(d == 0), stop=(d == 1)),
                             reads=[T("MT%d_%d" % (slot, d)), T("xs_sb%d" % s3)], writes=[T("ps_y")])
            if prev_y is not None:
                stage3b(prev_y)
            if do_y(tt):
                hf = hbf[slot]; hfn = "hbf%d" % slot
                S.op("act", lambda e: e.copy(hf, Hs[0]), reads=[T("H0")], writes=[T(hfn)])
                for d in range(2):
                    hin = hf if d == 0 else hinb[s3]
                    hinn = hfn if d == 0 else "hinb%d" % s3
                    for g in range(2):
                        S.op("pe", lambda e, g=g, hin=hin: e.matmul(yo[:, g * 512:(g + 1) * 512], CT[g][:, c0:c0 + 128], hin[:, g * 512:(g + 1) * 512], start=True, stop=True),
                             reads=[T("BCT"), T(hinn)], writes=[T("ps_yo")])
                    Ev = Eall[:, tt * 32 + d * 16: tt * 32 + d * 16 + 16]
                    dst = t1 if d == 0 else t2
                    S.op("dve", lambda e, dst=dst, Ev=Ev: e.tensor_tensor(h3(dst), h3(yo[:, :]), bc16(Ev), ALU.mult), reads=[T("ps_yo"), T("dts")], writes=[T("t1" if d == 0 else "t2")])
            if tt != NTT - 1:
                cd = cdall[:, tt * 32: tt * 32 + 16]
                for g in range(2):
                    S.op("pe", lambda e, g=g: e.matmul(st[:, g * 512:(g + 1) * 512], bt_sb[s3][:, g * 128:(g + 1) * 128], Xp2[slot][:, g * 512:(g + 1) * 512], start=True, stop=True),
                         reads=[T("bt_sb%d" % s3), T("Xp%d" % slot)], writes=[T("ps_st")])
                S.op("dve", lambda e: e.tensor_tensor(h3(Hs[0]), h3(Hs[0]), bc16(cd), ALU.mult), reads=[T("H0"), T("dts")], writes=[T("H0")])
                S.op("dve", lambda e: e.tensor_tensor(Hs[0], Hs[0], st[:, :], ALU.add), reads=[T("H0"), T("ps_st")], writes=[T("H0")])

        def stage3a(qi):
            tt = orderF[qi]; slot = qi % 2; s3 = qi % 3
            yps = PSW[1]
            S.op("pool", lambda e: e.tensor_tensor(t1, t1, t2, ALU.add), reads=[T("t1"), T("t2")], writes=[T("t1")])
            S.op("pool", lambda e: e.tensor_tensor(h3(t3), h3(xs_sb[s3]), bc16(dsk_bc), ALU.mult), reads=[T("xs_sb%d" % s3), T("rowbc")], writes=[T("t3")])
            S.op("pool", lambda e: e.tensor_tensor(t1, t1, t3, ALU.add), reads=[T("t1"), T("t3")], writes=[T("t1")])
            S.op("dve", lambda e: e.tensor_tensor(t1, t1, yps[:, :], ALU.add), reads=[T("t1"), T("ps_y")], writes=[T("t1")])
            S.op("dve", lambda e: e.tensor_tensor(t1, t1, szt[s3], ALU.mult), reads=[T("t1"), T("szt%d" % s3)], writes=[T("t1")])
            k = uid("s")
            ssq = ssb[:, (slot * 2):(slot * 2) + 1]; rstd = ssb[:, (slot * 2) + 1:(slot * 2) + 2]
            S.op("act", lambda e: e.activation(t2, t1, AF.Square, accum_out=ssq), reads=[T("t1")], writes=[T("t2"), T(k + "ss")])
            rms_rstd("dve", ssq, rstd, 1024.0, k + "ss", k + "rs")
            S.op("act", lambda e: e.activation(ygn2[slot], t1, AF.Copy, scale=rstd), reads=[T("t1"), T(k + "rs")], writes=[T("ygn%d" % slot)])

        NQ = len(orderF)
        stage_loads(0); stage1(0)
        prev_y = None
        for qi in range(NQ):
            if qi + 1 < NQ:
                stage_loads(qi + 1); stage1(qi + 1)
            conv_step(1)
            stage2(qi, prev_y)
            prev_y = None
            if do_y(orderF[qi]):
                stage3a(qi)
                prev_y = qi
        if prev_y is not None:
            stage3b(prev_y)
        if debug and layer == 0:
            S.dma("sp", lambda e: e.dma_start(out=dbg["mixT"], in_=mixT), reads=[T("mixT")], writes=[T("dbg_mixT")])
        if stop == "S%d" % layer:
            break

        S.barrier()
        A.release(mS2)
        mO = A.mark()
        wo = A.alloc(16 * D, BF16)
        wo3 = v3(wo, 16)
        conv_step(len(conv_pending))
        for i in range(4):
            S.dma("pool", lambda e, i=i: e.dma_start(out=wo3[:, :, i * 512:(i + 1) * 512], in_=wsrc(w_out[layer], i * 512, 512)), writes=[T("wo%d" % i)])
        ysc_sb = [A.alloc(8 * 128, BF16) for _ in range(2)]
        xold = [A.alloc(D) for _ in range(2)]
        tmpo = A.alloc(D)
        g1bc = [A.alloc(D), A.alloc(D) if not last else None]
        for m in range(2):
            if g1bc[m] is not None:
                S.dma("sp", lambda e, m=m: e.dma_start(out=g1bc[m], in_=g1_d[m]), reads=[T("g1d%d" % m)], writes=[T("g1bc")])
        ttsO = [tt for tt in range(NTT) if not (last and tt < 2)]
        def o_loads(qi):
            tt = ttsO[qi]; slot = qi % 2
            ys3 = v3(ysc_sb[slot], 8)
            S.dma("sp", lambda e: e.dma_start(out=ys3, in_=ysc_d[:, :, tt * 128:(tt + 1) * 128].rearrange("b p c -> p b c")), reads=[T("yscd")], writes=[T("ysc_sb%d" % slot)])
            rows, toks = tile_rows(layer, tt, layer == 0)
            load_tile(rows, toks, xold[slot], "xold%d" % slot)

        o_loads(0)
        for qi, tt in enumerate(ttsO):
            slot = qi % 2
            ys3 = v3(ysc_sb[slot], 8)
            if qi + 1 < len(ttsO):
                o_loads(qi + 1)
            for nb in range(4):
                pbk = (qi % 2) * 4 + nb
                for kc in range(16):
                    lhs = mixT3[:, kc, tt * 128:(tt + 1) * 128] if kc < 8 else ys3[:, kc - 8, :]
                    S.op("pe", lambda e, lhs=lhs, kc=kc, nb=nb, pbk=pbk: e.matmul(bank(pbk), lhs, wo3[:, kc, nb * 512:(nb + 1) * 512], start=(kc == 0), stop=(kc == 15)),
                         reads=[T("mixT"), T("ysc_sb%d" % slot), T("wo%d" % nb)], writes=[T("psb%d" % pbk)])
            gb = g1bc[1 if tt < 2 else 0]
            for nb in range(4):
                pbk = (qi % 2) * 4 + nb
                sl = slice(nb * 512, (nb + 1) * 512)
                S.op("dve", lambda e, pbk=pbk, sl=sl, gb=gb: e.tensor_tensor(tmpo[:, sl], bank(pbk), gb[:, sl], ALU.mult), reads=[T("psb%d" % pbk), T("g1bc")], writes=[T("tmpo")])
            S.op("pool", lambda e, slot=slot: e.tensor_tensor(xold[slot], xold[slot], tmpo, ALU.add), reads=[T("xold%d" % slot), T("tmpo")], writes=[T("xold%d" % slot)])
            drows, dtoks = tile_rows(layer, tt, False)
            store_tile(drows, dtoks, xold[slot], "xold%d" % slot)
        A.release(mO)
        A.release(mS)
        A.release(m_mix)
        if stop == "O%d" % layer:
            break

        tt_nat = list(range(NTT)) if not last else list(range(2, NTT))
        ftiles = [tt_nat[i:i + 6] for i in range(0, len(tt_nat), 6)]
        for fi, ft in enumerate(ftiles):
            ntt = len(ft)
            Tn = ntt * 128
            S.barrier()
            mF = A.mark()
            actT = A.alloc(44 * Tn, BF16)
            actT3 = v3(actT, 44)
            mF2 = A.mark()
            h2T = A.alloc(16 * Tn, BF16)
            h2T3 = v3(h2T, 16)
            NW = 4
            wring = [A.alloc(16 * 512, BF16) for _ in range(NW)]
            wstate["i"] = 0
            sgb = A.alloc(1024)
            sgs = [sgb[:, 0:512], sgb[:, 512:1024]]
            xts = [A.alloc(D), A.alloc(D)]
            xnbs = [A.alloc(D, BF16), A.alloc(D, BF16)]
            junk = sgb.bitcast(BF16); ssbs = [A.alloc(8), A.alloc(8)]

            def f1_load(si):
                tt = ft[si]
                S.dma("sp", lambda e: e.dma_start(out=xts[si % 2], in_=nat_rows(tt)), reads=[T("xr%d" % tt)], writes=[T("xt%d" % (si % 2))])

            def f1_A(si):
                norm_A(xts[si % 2], "xt%d" % (si % 2), junk, xnbs[si % 2], "xnb%d" % (si % 2), ssbs[si % 2])

            def f1_B(si):
                isctx = ft[si] < 2
                norm_B(xnbs[si % 2], "xnb%d" % (si % 2), ABs(6 if isctx else 4), ABs(7 if isctx else 5), h2T3, si * 128, "h2T", (0, 1))

            f1_load(0)
            if ntt > 1:
                f1_load(1)
            f1_A(0)
            for si in range(ntt):
                if si + 1 < ntt:
                    f1_A(si + 1)
                if si + 2 < ntt:
                    f1_load(si + 2)
                f1_B(si)
            subs = [(t0, min(512, Tn - t0)) for t0 in range(0, Tn, 512)]
            q = 0
            for j2 in range(11):
                wg, tg_ = wreload(j2, 512)
                wu, tu_ = wreload(11 + j2, 512)
                for j in range(4):
                    jb = j2 * 4 + j
                    for (t0, n) in subs:
                        pg = (q % 2) * 2; pu = pg + 1; q += 1
                        for kc in range(16):
                            S.op("pe", lambda e, kc=kc, wg=wg, j=j, pg=pg, t0=t0, n=n: e.matmul(bank(pg)[:, 0:n], wg[:, kc, j * 128:(j + 1) * 128], h2T3[:, kc, t0:t0 + n], start=(kc == 0), stop=(kc == 15)),
                                 reads=[T("h2Tk%d" % kc), tg_], writes=[T("psb%d" % pg)])
                        for kc in range(16):
                            S.op("pe", lambda e, kc=kc, wu=wu, j=j, pu=pu, t0=t0, n=n: e.matmul(bank(pu)[:, 0:n], wu[:, kc, j * 128:(j + 1) * 128], h2T3[:, kc, t0:t0 + n], start=(kc == 0), stop=(kc == 15)),
                                 reads=[T("h2Tk%d" % kc), tu_], writes=[T("psb%d" % pu)])
                        sg = sgs[q % 2]; sgn = "sg%d" % (q % 2)
                        S.op("act", lambda e, sg=sg, pg=pg, n=n: e.activation(sg[:, 0:n], bank(pg)[:, 0:n], AF.Silu), reads=[T("psb%d" % pg)], writes=[T(sgn)])
                        S.op("dve", lambda e, sg=sg, pu=pu, n=n, jb=jb, t0=t0: e.tensor_tensor(actT3[:, jb, t0:t0 + n], sg[:, 0:n], bank(pu)[:, 0:n], ALU.mult),
                             reads=[T(sgn), T("psb%d" % pu)], writes=[T("actT")])
                if layer == 0 and ada_pending:
                    ada_ctr[0] += 1
                    if ada_ctr[0] % 3 != 0:
                        bi_, blk_, i_ = ada_pending.pop(0)
                        ada_step(1, bi_, blk_, i_, bank(4), "ps_mp1")
            S.barrier()
            A.release(mF2)
            wd = [A.alloc(44 * 256, BF16) for _ in range(2)]
            xstage = A.alloc(ntt * D)
            xst3 = v3(xstage, ntt)
            oTs = [A.alloc(512) for _ in range(2)]
            fngbc = None
            if last:
                fngbc = A.alloc(D)
                S.dma("sp", lambda e: e.dma_start(out=fngbc, in_=fng_in[0].partition_broadcast(128)), writes=[T("fngbc")])
                ssb = A.alloc(8)
                junk = A.alloc(D)
            for si, tt in enumerate(ft):
                S.dma("sp", lambda e, si=si, tt=tt: e.dma_start(out=xst3[:, si, :], in_=nat_rows(tt)), reads=[T("xr%d" % tt)], writes=[T("xst%d" % si)])
            q = 0
            wdsrc = w_down[layer].rearrange("(k p) n -> p k n", p=128)

            def f3_tail(g):
                (ot, otn, pT, t0, n, nb) = g
                for u in range(n // 128):
                    S.op("pe", lambda e, u=u: e.transpose(bank(pT)[:, u * 128:(u + 1) * 128], ot[:, u * 128:(u + 1) * 128], IDF), reads=[T(otn), T("cst")], writes=[T("psb%d" % pT)])
                for u in range(n // 128):
                    si = (t0 // 128) + u
                    S.op("dve", lambda e, si=si, u=u: e.tensor_tensor(xst3[:, si, nb * 128:(nb + 1) * 128], xst3[:, si, nb * 128:(nb + 1) * 128], bank(pT)[:, u * 128:(u + 1) * 128], ALU.add),
                         reads=[T("xst%d" % si), T("psb%d" % pT)], writes=[T("xst%d" % si)])

            prev_g = None
            for n8 in range(8):
                wdt_ = wd[n8 % 2]; wdn = "wd%d" % (n8 % 2)
                wd3 = v3(wdt_, 44)
                S.dma("sp", lambda e: e.dma_start(out=wdt_, in_=wds_d[n8]), reads=[T("wdsd%d_%d" % (n8, kq_)) for kq_ in range(4)], writes=[T(wdn)])
                for nbi in range(2):
                    nb = n8 * 2 + nbi
                    for (t0, n) in subs:
                        pb = q % 2; q += 1
                        for jb in range(44):
                            S.op("pe", lambda e, jb=jb: e.matmul(bank(pb)[:, 0:n], wd3[:, jb, nbi * 128:(nbi + 1) * 128], actT3[:, jb, t0:t0 + n], start=(jb == 0), stop=(jb == 43)),
                                 reads=[T("actT"), T(wdn)], writes=[T("psb%d" % pb)])
                        ot = oTs[q % 2]; otn = "oTs%d" % (q % 2)
                        for u in range(n // 128):
                            tt = ft[(t0 // 128) + u]
                            m = 1 if tt < 2 else 0
                            S.op("act", lambda e, u=u, m=m: e.activation(ot[:, u * 128:(u + 1) * 128], bank(pb)[:, u * 128:(u + 1) * 128], AF.Copy, scale=G2[:, m * 16 + nb: m * 16 + nb + 1]),
                                 reads=[T("psb%d" % pb), T("modp")], writes=[T(otn)])
                        if prev_g is not None:
                            f3_tail(prev_g)
                        prev_g = (ot, otn, 2 + (q % 2), t0, n, nb)
            f3_tail(prev_g)
            for si, tt in enumerate(ft):
                if not last:
                    S.dma("sp", lambda e, si=si, tt=tt: e.dma_start(out=nat_rows(tt), in_=xst3[:, si, :]), reads=[T("xst%d" % si)], writes=[T("xr%d" % tt)])
                else:
                    k = uid("f")
                    ss = ssb[:, 0:1]; rstd = ssb[:, 1:2]
                    xs_ = xst3[:, si, :]
                    S.op("act", lambda e, xs_=xs_: e.activation(junk, xs_, AF.Square, accum_out=ss), reads=[T("xst%d" % si)], writes=[T("junk"), T(k + "ss")])
                    rms_rstd("dve", ss, rstd, float(D), k + "ss", k + "rs")
                    S.op("dve", lambda e, xs_=xs_: e.scalar_tensor_tensor(xs_, xs_, rstd, fngbc, ALU.mult, ALU.mult), reads=[T("xst%d" % si), T(k + "rs"), T("fngbc")], writes=[T("xst%d" % si)])
                    r0 = (tt - 2) * 128
                    S.dma("sp", lambda e, si=si, r0=r0: e.dma_start(out=out[r0:r0 + 128, :], in_=xst3[:, si, :]), reads=[T("xst%d" % si)], writes=[T("out")])
            A.release(mF)
        A.release(m_layer)
        if stop == "F%d" % layer:
            break

    S.finalize()
    es.close()
    return nc, S, A


def _host_inputs(inp):
    f = np.float32
    cst = np.zeros((128, 768), f)
    s = np.arange(128)[:, None]; l = np.arange(128)[None, :]
    cst[:, 0:128] = np.eye(128, dtype=f)
    cst[:, 128:256] = (s <= l)
    cst[:, 256:384] = (s >= l)
    cst[:, 384:512] = np.where(s <= l, 0.0, NEG)
    cst[:, 512:640] = np.where(s >= l, 0.0, NEG)
    cst[:, 640:768] = 1.0
    idb = np.eye(128).astype(ml_dtypes.bfloat16)

    def pl(v):
        return np.ascontiguousarray(np.asarray(v, f).reshape(-1, 128).T)

    vecs = np.zeros((128, 2 * NV), f)
    rowv = np.zeros((1, 160), f)
    for l_ in range(2):
        o = l_ * NV
        vecs[:, o + OFF_ADAB:o + OFF_ADAB + 96] = pl(inp["ada_b"][l_])
        vecs[:, o + OFF_G1:o + OFF_G1 + 16] = pl(inp["mix_norm_g"][l_])
        vecs[:, o + OFF_G2:o + OFF_G2 + 16] = pl(inp["ffn_norm_g"][l_])
        vecs[:, o + OFF_GS:o + OFF_GS + 8] = pl(inp["ssd_norm_g"][l_])
        for k in range(5):
            vecs[:, o + OFF_CW + k * 12:o + OFF_CW + (k + 1) * 12] = pl(inp["ssd_conv_w"][l_, k])
        vecs[:, o + OFF_CB:o + OFF_CB + 12] = pl(inp["ssd_conv_b"][l_])
        for k in range(3):
            vecs[:, o + OFF_SCW + k * 8:o + OFF_SCW + (k + 1) * 8] = pl(inp["sc_conv_w"][l_, k])
        rowv[0, l_ * 80:l_ * 80 + 32] = np.asarray(inp["ssd_dt_bias"][l_], f).reshape(32)
        rowv[0, l_ * 80 + 32:l_ * 80 + 64] = np.asarray(inp["ssd_a_log"][l_], f).reshape(32)
        rowv[0, l_ * 80 + 64:l_ * 80 + 80] = np.asarray(inp["ssd_d"][l_], f)
    shared = {
        "ada_w": np.ascontiguousarray(inp["ada_w"], f), "ada_b": np.ascontiguousarray(inp["ada_b"], f),
        "w_in": np.ascontiguousarray(inp["w_in"], f), "w_out": np.ascontiguousarray(inp["w_out"], f),
        "w_gate": np.ascontiguousarray(inp["w_gate"], f), "w_up": np.ascontiguousarray(inp["w_up"], f),
        "w_down": np.ascontiguousarray(inp["w_down"], f),
        "vecs": vecs, "rowv": rowv, "fng": np.asarray(inp["final_norm_g"], f).reshape(1, D),
        "cst": cst, "idb": idb,
    }
    maps = []
    cc = np.asarray(inp["c_ctx"], f)
    for b in range(8):
        cvb = np.zeros((128, 16, 2), f)
        cvb[:, :, 0] = pl(inp["c"][b])
        cvb[:, :, 1] = pl(cc)
        m = dict(shared)
        m["x"] = np.ascontiguousarray(inp["x"][b], f)
        m["ctx"] = np.ascontiguousarray(inp["ctx"][b], f)
        m["cv"] = cvb.reshape(128, 32)
        maps.append(m)
    return maps


def kernel(**inputs):
    nc, S, A = build_program()
    maps = _host_inputs(inputs)
    res = run_bass_kernel_spmd(nc, maps, core_ids=list(range(8)))
    return np.stack([np.asarray(r["out"], np.float32) for r in res.results], axis=0)
```

```python
import numpy as np
import ml_dtypes
import concourse.bass as bass
import concourse.mybir as mybir
from contextlib import ExitStack
from concourse.bass_utils import run_bass_kernel_spmd

F32 = mybir.dt.float32
BF16 = mybir.dt.bfloat16
AF = mybir.ActivationFunctionType
ALU = mybir.AluOpType
AX = mybir.AxisListType

SAME_ENGINE_SYNC = True


class Tile:
    __slots__ = ("name", "last_w", "rd_eng", "rd_dma", "excl")

    def __init__(self, name):
        self.name = name
        self.excl = name.startswith("ps") or name == "mp_ps"
        self.last_w = None
        self.rd_eng = {}
        self.rd_dma = []


class Op:
    __slots__ = ("eng", "fn", "deps", "is_dma", "needs_inc", "inc_val", "sem_k", "sem_val", "prev_val", "gidx")

    def __init__(self, eng, fn, is_dma):
        self.eng = eng
        self.fn = fn
        self.is_dma = is_dma
        self.deps = []
        self.needs_inc = False
        self.inc_val = 0
        self.sem_k = -1
        self.sem_val = 0
        self.prev_val = 0


class _Rec:
    def __init__(self):
        self.call = None

    def __getattr__(self, name):
        def f(*a, **k):
            self.call = (name, a, k)
            return self
        return f


class Sched:
    ENGS = ("pe", "act", "dve", "pool", "sp")

    def __init__(self, nc, n_dma_sems=40):
        self.nc = nc
        self.ops = {e: [] for e in self.ENGS}
        self.all = []
        self.n_dma_sems = n_dma_sems
        self.tiles = {}
        self._dmas_since_bar = []
        self._bar_pending = {}

    def T(self, name):
        t = self.tiles.get(name)
        if t is None:
            t = Tile(name)
            self.tiles[name] = t
        return t

    def _track(self, o, reads, writes):
        deps = {}

        def add(p, kind):
            if p is None or p is o:
                return
            if p.eng == o.eng and not p.is_dma and not o.is_dma:
                if o.eng == "pe":
                    return
                if kind in ("war", "waw", "rar") or not SAME_ENGINE_SYNC:
                    return
            deps[id(p)] = p

        for t in reads:
            add(t.last_w, "raw")
            if t.excl:
                for r in t.rd_eng.values():
                    add(r, "rar")
        for t in writes:
            add(t.last_w, "waw")
            for r in t.rd_eng.values():
                add(r, "war")
            for r in t.rd_dma:
                add(r, "war")
        for t in reads:
            if o.is_dma:
                t.rd_dma.append(o)
            else:
                t.rd_eng[o.eng] = o
        for t in writes:
            t.last_w = o
            t.rd_eng = {}
            t.rd_dma = []
        if o.eng in self._bar_pending:
            for p in self._bar_pending.pop(o.eng):
                if p is not o and not (p.eng == o.eng and not p.is_dma and not o.is_dma):
                    deps[id(p)] = p
        if o.is_dma:
            self._dmas_since_bar.append(o)
        o.deps = list(deps.values())
        for p in o.deps:
            if not p.is_dma:
                p.needs_inc = True
        self.ops[o.eng].append(o)
        self.all.append(o)

    def op(self, eng, fn, reads=(), writes=()):
        r = _Rec(); fn(r)
        o = Op(eng, r.call, False)
        self._track(o, reads, writes)
        return o

    def dma(self, eng, fn, reads=(), writes=()):
        r = _Rec(); fn(r)
        o = Op(eng, r.call, True)
        self._track(o, reads, writes)
        return o

    def barrier(self):
        B = []
        for e in self.ENGS:
            for o in reversed(self.ops[e]):
                if not o.is_dma:
                    B.append(o)
                    break
        B.extend(self._dmas_since_bar)
        self._dmas_since_bar = []
        self._bar_pending = {e: B for e in self.ENGS}

    def finalize(self, final_wait_eng="sp"):
        nc = self.nc
        with ExitStack() as es:
            prog = {e: es.enter_context(nc.semaphore("pg_" + e)) for e in ("pe", "act", "dve", "pool")}
            dsem = [es.enter_context(nc.semaphore("dm%d" % i)) for i in range(self.n_dma_sems)]
            cnt = {e: 0 for e in self.ENGS}
            uses = [0] * self.n_dma_sems
            k = 0
            for o in self.all:
                if o.is_dma:
                    o.sem_k = k
                    o.prev_val = uses[k] * 16
                    uses[k] += 1
                    o.sem_val = uses[k] * 16
                    k = (k + 1) % self.n_dma_sems
                elif o.needs_inc:
                    cnt[o.eng] += 1
                    o.inc_val = cnt[o.eng]
            self.stats = dict(cnt)
            self.stats.update({"n_" + e: len(self.ops[e]) for e in self.ENGS})
            final_waits = [(dsem[i], uses[i] * 16) for i in range(self.n_dma_sems) if uses[i] > 0]

            def emit_engine(e, name):
                waited = {}

                def wait(sem, val, key):
                    if waited.get(key, 0) >= val:
                        return
                    e.wait_ge(sem, val)
                    waited[key] = val

                for o in self.ops[name]:
                    for p in o.deps:
                        if p.is_dma:
                            wait(dsem[p.sem_k], p.sem_val, ("d", p.sem_k))
                        else:
                            wait(prog[p.eng], p.inc_val, ("p", p.eng))
                    if o.is_dma and o.prev_val > 0:
                        wait(dsem[o.sem_k], o.prev_val, ("d", o.sem_k))
                    ins = getattr(e, o.fn[0])(*o.fn[1], **o.fn[2])
                    if o.is_dma:
                        ins.then_inc(dsem[o.sem_k], 16)
                    elif o.needs_inc:
                        ins.then_inc(prog[name], 1)
                if name == final_wait_eng:
                    for sem, val in final_waits:
                        wait(sem, val, ("f", id(sem)))

            with nc.Block() as block:
                @block.tensor
                def _(e):
                    emit_engine(e, "pe")

                @block.scalar
                def _(e):
                    emit_engine(e, "act")

                @block.vector
                def _(e):
                    emit_engine(e, "dve")

                @block.gpsimd
                def _(e):
                    emit_engine(e, "pool")

                @block.sync
                def _(e):
                    emit_engine(e, "sp")


class Arena:
    def __init__(self, ap_f32, ncols):
        self.ap = ap_f32
        self.ncols = ncols
        self.off = 0
        self.peak = 0

    def mark(self):
        return self.off

    def release(self, m):
        self.off = m

    def alloc(self, cols, dtype=F32):
        if dtype == BF16:
            n32 = (cols + 1) // 2
        else:
            n32 = cols
        n32 = (n32 + 7) // 8 * 8
        a = self.ap[:, self.off:self.off + n32]
        self.off += n32
        self.peak = max(self.peak, self.off)
        assert self.off <= self.ncols, "arena overflow %d > %d" % (self.off, self.ncols)
        if dtype == BF16:
            return a.bitcast(BF16)[:, 0:cols]
        return a[:, 0:cols]


D = 2048
NTT = 18
NV = 232
NEG = -30000.0
EPS = 1e-6
OFF_ADAB, OFF_G1, OFF_G2, OFF_GS, OFF_CW, OFF_CB, OFF_SCW = 0, 96, 112, 128, 136, 196, 208
TGROUPS = [(0, 256), (256, 512), (768, 512), (1280, 512), (1792, 512)]


def pcol(tok):
    return tok + 2 if tok < 256 else tok + 6


PADW = 2312


def build_program(debug=False, stop=None, acols=51200):
    nc = bass.Bass("TRN2", target_bir_lowering=False)

    def din(name, shape, dt=F32):
        return nc.dram_tensor(name, shape, dt, kind="ExternalInput").ap()

    x_in = din("x", [2048, D]); ctx_in = din("ctx", [256, D]); cv_in = din("cv", [128, 32])
    ada_w = din("ada_w", [2, D, 6 * D]); adab_in = din("ada_b", [2, 6 * D])
    w_in = din("w_in", [2, D, 5664]); w_out = din("w_out", [2, D, D])
    w_gate = din("w_gate", [2, D, 5632]); w_up = din("w_up", [2, D, 5632]); w_down = din("w_down", [2, 5632, D])
    vecs_in = din("vecs", [128, 2 * NV]); rowv_in = din("rowv", [1, 160]); fng_in = din("fng", [1, D])
    cst_in = din("cst", [128, 768]); idb_in = din("idb", [128, 128], BF16)
    out = nc.dram_tensor("out", [2048, D], F32, kind="ExternalOutput").ap()
    kind_s = "ExternalOutput" if debug else "Internal"

    def dscr(name, shape, dt):
        return nc.dram_tensor(name, shape, dt, kind=kind_s).ap()

    xres = dscr("xres", [2304, D], F32)
    sz_d = dscr("sz_d", [NTT, 128, 1024], BF16)
    xs_d = dscr("xs_d", [NTT, 128, 1024], BF16)
    bt_d = dscr("bt_d", [NTT, 128, 256], BF16)
    ysc_d = dscr("ysc_d", [8, 128, 2304], BF16)
    acs_d = dscr("acs_d", [NTT, 32, 128], F32)
    hinb_d = dscr("hinb_d", [NTT, 128, 1024], BF16)
    g1_d = dscr("g1_d", [2, 128, D], F32)
    wgs_d = dscr("wgs_d", [22, 128, 16 * 512], BF16)
    wds_d = dscr("wds_d", [8, 128, 44 * 256], BF16)
    dbg = {}
    if debug:
        dbg["hT"] = dscr("hT_d", [128, 16 * 2304], BF16)
        dbg["modp"] = dscr("modp_d", [128, 160], F32)
        dbg["mixT"] = dscr("mixT_d", [128, 8 * 2304], BF16)
        dbg["dts"] = dscr("dts_d", [128, 4 * NTT * 32], F32)
        dbg["bct"] = dscr("bct_d", [128, 2 * 2 * PADW], BF16)

    es = ExitStack()
    big = es.enter_context(nc.sbuf_tensor("arena", [128, acols], F32))
    A = Arena(big, acols)
    PSW = [es.enter_context(nc.psum_tensor("psw%d" % i, [128, 1024], F32)) for i in range(4)]

    def bank(i):
        return PSW[i // 2][:, (i % 2) * 512:(i % 2) * 512 + 512]

    S = Sched(nc)
    T = S.T
    cnt = [0]

    def uid(p):
        cnt[0] += 1
        return "%s_%d" % (p, cnt[0])

    def v3(ap, a):
        return ap.rearrange("p (a b) -> p a b", a=a)

    cst = A.alloc(768); vecs = A.alloc(2 * NV); rowbc = A.alloc(160); cv = A.alloc(32)
    idb = A.alloc(128, BF16)
    sT = A.alloc(32, BF16)
    IDF = cst[:, 0:128]; TRIU = cst[:, 128:256]; TRIL = cst[:, 256:384]
    NEGM = [cst[:, 384:512], cst[:, 512:640]]; ONES = cst[:, 640:768]
    S.dma("sp", lambda e: e.dma_start(out=cst, in_=cst_in), writes=[T("cst")])
    S.dma("sp", lambda e: e.dma_start(out=vecs, in_=vecs_in), writes=[T("vecs")])
    S.dma("sp", lambda e: e.dma_start(out=rowbc, in_=rowv_in[0].partition_broadcast(128)), writes=[T("rowbc")])
    S.dma("sp", lambda e: e.dma_start(out=cv, in_=cv_in), writes=[T("cv")])
    S.dma("sp", lambda e: e.dma_start(out=idb, in_=idb_in), writes=[T("idb")])
    S.op("act", lambda e: e.activation(sT, cv, AF.Silu), reads=[T("cv")], writes=[T("sT")])

    sT3g = v3(sT, 16)
    PBLOCKS = [0, 1, 3, 4, 5]

    def ada_step(L, bi, blk, i, mp, mpn):
        wt, wtile = wload(wsrc(ada_w[L], blk * D + i * 512, 512), 512)
        for j in range(4):
            ch = i * 4 + j
            o = (bi * 16 + ch) * 2
            for kc in range(16):
                S.op("pe", lambda e, j=j, kc=kc, o=o: e.matmul(mp[:, o:o + 2], wt[:, kc, j * 128:(j + 1) * 128], sT3g[:, kc, :], start=(kc == 0), stop=(kc == 15)),
                     reads=[wtile, T("sT")], writes=[T(mpn)])

    ada_pending = [(bi, blk, i) for bi, blk in enumerate(PBLOCKS) for i in range(4)]
    ada_ctr = [0]

    NW = 4
    wring = []
    wstate = {"i": 0}

    def wload(src3, width, kchunks=16):
        nonlocal_ring = None
        i = wstate["i"]; wstate["i"] += 1
        nw = len(wring)
        buf = wring[i % nw]
        t = T("wring%d" % (i % nw))
        dst = buf[:, 0:kchunks * width].rearrange("p (k n) -> p k n", k=kchunks)
        S.dma("pool", lambda e: e.dma_start(out=dst, in_=src3), writes=[t])
        return dst, t

    def wreload(tile_id, width, kchunks=16):
        i = wstate["i"]; wstate["i"] += 1
        nw = len(wring)
        buf = wring[i % nw]
        t = T("wring%d" % (i % nw))
        flat = buf[:, 0:kchunks * width]
        S.dma("sp", lambda e: e.dma_start(out=flat, in_=wgs_d[tile_id]), reads=[T("wgsd%d" % tile_id)], writes=[t])
        return flat.rearrange("p (k n) -> p k n", k=kchunks), t

    def wsave(ap3, t, tile_id, width, kchunks=16):
        flat = ap3.rearrange("p k n -> p (k n)")
        S.dma("sp", lambda e: e.dma_start(out=wgs_d[tile_id], in_=flat), reads=[t], writes=[T("wgsd%d" % tile_id)])

    def wsrc(w2d, c0, width):
        return w2d.rearrange("(k p) n -> p k n", p=128)[:, :, c0:c0 + width]

    def tile_rows(layer, tt, first_src):
        if tt < 2:
            src = ctx_in if first_src else xres
            return [(src[tt * 128:(tt + 1) * 128, :], 0, 128)], ["xr%d" % tt]
        j0 = (tt - 2) * 128
        if layer == 0:
            if first_src:
                return [(x_in[j0:j0 + 128, :], 0, 128)], ["xr%d" % tt]
            return [(xres[256 + j0:256 + j0 + 128, :], 0, 128)], ["xr%d" % tt]
        lat = xres[256:2304, :].rearrange("(r w) f -> w r f", w=64)
        w0 = j0 // 32
        return [(lat[w0 + i], 32 * i, 32) for i in range(4)], ["xr%d" % t for t in range(2, NTT)]

    def nat_rows(tt):
        return xres[tt * 128:(tt + 1) * 128, :]

    def rms_rstd(eng_name, ss, rstd, n, ssname, rname):
        S.op("dve", lambda e: e.tensor_scalar(rstd, ss, 1.0 / n, EPS, ALU.mult, ALU.add), reads=[T(ssname)], writes=[T(rname)])
        S.op("act", lambda e: e.activation(rstd, rstd, AF.Ln), reads=[T(rname)], writes=[T(rname)])
        S.op("act", lambda e: e.activation(rstd, rstd, AF.Exp, scale=-0.5), reads=[T(rname)], writes=[T(rname)])

    def norm_A(xt, xt_name, junk, xnb, xnname, ssb):
        k = uid("n")
        ss = ssb[:, 0:1]; rstd = ssb[:, 1:2]
        S.op("act", lambda e: e.activation(junk, xt, AF.Square, accum_out=ss), reads=[T(xt_name)], writes=[T(k + "ss")])
        rms_rstd("dve", ss, rstd, float(D), k + "ss", k + "rs")
        S.op("act", lambda e: e.activation(xnb, xt, AF.Copy, scale=rstd), reads=[T(xt_name), T(k + "rs")], writes=[T(xnname)])

    def norm_B(xnb, xnname, Aap, Bap, hT3, col0, hname, psb):
        for half in range(2):
            pb = psb[half]
            pbn = "psb%d" % pb
            Pb = bank(pb).bitcast(BF16)
            for j in range(8):
                kc = half * 8 + j
                S.op("pe", lambda e, kc=kc, j=j: e.transpose(Pb[:, j * 128:(j + 1) * 128], xnb[:, kc * 128:(kc + 1) * 128], idb),
                     reads=[T(xnname), T("idb")], writes=[T(pbn)])
            for j in range(8):
                kc = half * 8 + j
                dst = hT3[:, kc, col0:col0 + 128]
                if True:
                    S.op("dve", lambda e, kc=kc, j=j, dst=dst: e.tensor_scalar(dst, Pb[:, j * 128:(j + 1) * 128], Aap[:, kc:kc + 1], Bap[:, kc:kc + 1], ALU.mult, ALU.add),
                         reads=[T(pbn), T("modp")], writes=[T(hname + "k%d" % kc)])
                else:
                    S.op("act", lambda e, kc=kc, j=j, dst=dst: e.activation(dst, Pb[:, j * 128:(j + 1) * 128], AF.Identity, bias=Bap[:, kc:kc + 1], scale=Aap[:, kc:kc + 1]),
                         reads=[T(pbn), T("modp")], writes=[T(hname + "k%d" % kc)])

    def load_tile(rows, toks, xt, name):
        for (src, p0, n) in rows:
            S.dma("sp", lambda e, src=src, p0=p0, n=n: e.dma_start(out=xt[p0:p0 + n, :], in_=src), reads=[T(t) for t in toks], writes=[T(name)])

    def store_tile(rows, toks, xt, name):
        for (dst, p0, n) in rows:
            S.dma("sp", lambda e, dst=dst, p0=p0, n=n: e.dma_start(out=dst, in_=xt[p0:p0 + n, :]), reads=[T(name)], writes=[T(t) for t in toks])

    conv_pending = []

    def conv_fill(L):
        for j2 in range(11):
            conv_pending.append(("g", L, j2)); conv_pending.append(("u", L, j2))
        for n8 in range(8):
            for kq in range(4):
                conv_pending.append(("d", L, n8, kq))

    def conv_step(n=1):
        for _ in range(n):
            if not conv_pending:
                return
            it = conv_pending.pop(0)
            if it[0] in ("g", "u"):
                _, L, j2 = it
                w2 = w_gate[L] if it[0] == "g" else w_up[L]
                tid = j2 if it[0] == "g" else 11 + j2
                S.dma("pool", lambda e: e.dma_start(out=wgs_d[tid].rearrange("p (k n) -> p k n", k=16), in_=wsrc(w2, j2 * 512, 512)),
                      writes=[T("wgsd%d" % tid), T("convchain")])
            else:
                _, L, n8, kq = it
                src = w_down[L].rearrange("(k p) n -> p k n", p=128)[:, kq * 11:(kq + 1) * 11, n8 * 256:(n8 + 1) * 256]
                dst = wds_d[n8].rearrange("p (k n) -> p k n", k=44)[:, kq * 11:(kq + 1) * 11, :]
                S.dma("pool", lambda e: e.dma_start(out=dst, in_=src), writes=[T("wdsd%d_%d" % (n8, kq)), T("convchain")])

    for layer in range(2):
        last = layer == 1
        vo = layer * NV
        m_layer = A.mark()
        modp = A.alloc(160)
        AB = A.alloc(8 * 16)
        G2 = A.alloc(32)

        def ABs(i):
            return AB[:, i * 16:(i + 1) * 16]

        S.barrier()
        mA = A.mark()
        wring = [A.alloc(16 * 512, BF16) for _ in range(NW)]
        wstate["i"] = 0
        sTbc = [A.alloc(16 * 128, BF16), A.alloc(16 * 128, BF16)]
        adab_bc = A.alloc(D)
        g1st = A.alloc(D)
        sT3 = v3(sT, 16)
        for m in range(2):
            S.op("dve", lambda e, m=m: e.tensor_copy(v3(sTbc[m], 16), sT3[:, :, m:m + 1].broadcast_to([128, 16, 128])),
                 reads=[T("sT")], writes=[T("sTbc")])
        S.dma("sp", lambda e: e.dma_start(out=adab_bc, in_=adab_in[layer, 2 * D:3 * D].partition_broadcast(128)), writes=[T("adab_bc")])
        aw = ada_w[layer]
        pblocks = PBLOCKS
        if layer == 0:
            mp_ps = bank(0); mpn = "mp_ps"
            for bi, blk in enumerate(pblocks):
                for i in range(4):
                    ada_step(0, bi, blk, i, mp_ps, mpn)
        else:
            mp_ps = bank(4); mpn = "ps_mp1"
            while ada_pending:
                bi, blk, i = ada_pending.pop(0)
                ada_step(1, bi, blk, i, mp_ps, mpn)
        for bi, blk in enumerate(pblocks):
            S.op("dve", lambda e, bi=bi, blk=blk: e.tensor_tensor(
                v3(modp[:, bi * 32:(bi + 1) * 32], 16), v3(mp_ps[:, bi * 32:(bi + 1) * 32], 16),
                vecs[:, vo + OFF_ADAB + blk * 16: vo + OFF_ADAB + (blk + 1) * 16].unsqueeze(2).broadcast_to([128, 16, 2]), ALU.add),
                reads=[T(mpn), T("vecs")], writes=[T("modp")])
        mp3 = modp.rearrange("p (b k m) -> p b k m", b=5, k=16)
        for m in range(2):
            for n_i, (gofs, bsh, bsc) in enumerate([(OFF_G1, 0, 1), (OFF_G2, 2, 3)]):
                Aap = ABs(n_i * 4 + m * 2); Bap = ABs(n_i * 4 + m * 2 + 1)
                gv = vecs[:, vo + gofs: vo + gofs + 16]
                S.op("dve", lambda e, Aap=Aap, gv=gv, bsc=bsc, m=m: e.scalar_tensor_tensor(Aap, mp3[:, bsc, :, m], 1.0, gv, ALU.add, ALU.mult),
                     reads=[T("modp"), T("vecs")], writes=[T("modp")])
                S.op("dve", lambda e, Bap=Bap, bsh=bsh, m=m: e.tensor_copy(Bap, mp3[:, bsh, :, m]), reads=[T("modp")], writes=[T("modp")])
            S.op("dve", lambda e, m=m: e.tensor_copy(G2[:, m * 16:(m + 1) * 16], mp3[:, 4, :, m]), reads=[T("modp")], writes=[T("modp")])
        if debug and layer == 0:
            S.dma("sp", lambda e: e.dma_start(out=dbg["modp"], in_=modp), reads=[T("modp")], writes=[T("dbg_modp")])
        for i in range(4):
            wt, wtile = wload(wsrc(aw, 2 * D + i * 512, 512), 512)
            for m in range(2):
                if m == 1 and last:
                    continue
                pb = 2 + (i * 2 + m) % 2
                st3 = v3(sTbc[m], 16)
                for kc in range(16):
                    S.op("pe", lambda e, wt=wt, kc=kc, pb=pb, st3=st3: e.matmul(bank(pb), st3[:, kc, :], wt[:, kc, :], start=(kc == 0), stop=(kc == 15)),
                         reads=[wtile, T("sTbc")], writes=[T("psb%d" % pb)])
                gname = "g1st%d" % m
                S.op("dve", lambda e, pb=pb, i=i: e.tensor_tensor(g1st[:, 0:512], bank(pb), adab_bc[:, i * 512:(i + 1) * 512], ALU.add),
                     reads=[T("psb%d" % pb), T("adab_bc")], writes=[T("g1st")])
                S.dma("sp", lambda e, m=m, i=i: e.dma_start(out=g1_d[m, :, i * 512:(i + 1) * 512], in_=g1st[:, 0:512]), reads=[T("g1st")], writes=[T("g1d%d" % m)])
        A.release(mA)
        if stop == "A%d" % layer:
            break

        S.barrier()
        m_mix = A.mark()
        dts = A.alloc(4 * NTT * 32)
        biasall = dts[:, 0:576]; Eall = dts[:, 576:1152]; wstall = dts[:, 1152:1728]; cdall = dts[:, 1728:2304]
        a_bc = A.alloc(32)
        BCT = A.alloc(4 * PADW, BF16)
        BT = [BCT[:, g * PADW:(g + 1) * PADW] for g in range(2)]
        CT = [BCT[:, (2 + g) * PADW:(3 + g) * PADW] for g in range(2)]
        S.op("act", lambda e: e.activation(a_bc, rowbc[:, layer * 80 + 32: layer * 80 + 64], AF.Exp), reads=[T("rowbc")], writes=[T("a_bc")])
        S.op("dve", lambda e: e.tensor_scalar(a_bc, a_bc, -1.0, None, ALU.mult), reads=[T("a_bc")], writes=[T("a_bc")])
        dtb_bc = rowbc[:, layer * 80: layer * 80 + 32]
        dsk_bc = rowbc[:, layer * 80 + 64: layer * 80 + 80]

        mP = A.mark()
        hT = A.alloc(16 * 2304, BF16)
        hT3 = v3(hT, 16)
        mN = A.mark()
        xts = [A.alloc(D), A.alloc(D)]
        junk = A.alloc(D, BF16); xnbs = [A.alloc(D, BF16), A.alloc(D, BF16)]; ssbs = [A.alloc(8), A.alloc(8)]
        tts = list(range(NTT))

        def n1_load(tt):
            rows, toks = tile_rows(layer, tt, layer == 0)
            load_tile(rows, toks, xts[tt % 2], "xt%d" % (tt % 2))

        def n1_A(tt):
            norm_A(xts[tt % 2], "xt%d" % (tt % 2), junk, xnbs[tt % 2], "xnb%d" % (tt % 2), ssbs[tt % 2])

        def n1_B(tt):
            isctx = tt < 2
            norm_B(xnbs[tt % 2], "xnb%d" % (tt % 2), ABs(2 if isctx else 0), ABs(3 if isctx else 1), hT3, tt * 128, "hT", (0, 1))

        n1_load(0); n1_load(1); n1_A(0)
        for tt in tts:
            if tt + 1 < NTT:
                n1_A(tt + 1)
            if tt + 2 < NTT:
                n1_load(tt + 2)
            n1_B(tt)
        A.release(mN)
        if debug and layer == 0:
            S.dma("sp", lambda e: e.dma_start(out=dbg["hT"], in_=hT), reads=[T("hTk%d" % k_) for k_ in range(16)], writes=[T("dbg_hT")])
        if stop == "N%d" % layer:
            break
        S.barrier()
        win = w_in[layer]
        conv_fill(layer)
        mz = A.mark()
        NW = 3
        wring = [A.alloc(16 * 512, BF16) for _ in range(NW)]
        wstate["i"] = 0
        szst = [A.alloc(512, BF16) for _ in range(2)]
        q = 0
        for i in range(2):
            wt, wtile = wload(wsrc(win, i * 512, 512), 512)
            conv_step(2)
            for tt in tts:
                if last and tt < 2:
                    continue
                pb = q % 4; q += 1
                for kc in range(16):
                    S.op("pe", lambda e, kc=kc, tt=tt, wt=wt, pb=pb: e.matmul(bank(pb), hT3[:, kc, tt * 128:(tt + 1) * 128], wt[:, kc, :], start=(kc == 0), stop=(kc == 15)),
                         reads=[T("hTk%d" % kc), wtile], writes=[T("psb%d" % pb)])
                sb = szst[q % 2]; sbn = "szst%d" % (q % 2)
                S.op("act", lambda e, sb=sb, pb=pb: e.activation(sb, bank(pb), AF.Silu), reads=[T("psb%d" % pb)], writes=[T(sbn)])
                S.dma("act", lambda e, sb=sb, tt=tt, i=i: e.dma_start(out=sz_d[tt, :, i * 512:(i + 1) * 512], in_=sb), reads=[T(sbn)], writes=[T("szd%d" % tt)])
        A.release(mz)
        S.barrier()
        NW = 4
        mring = A.mark()
        wring = [A.alloc(16 * 256, BF16) for _ in range(NW)]
        wstate["i"] = 0
        mdt = A.alloc
        mdtm = A.mark()
        wdt = A.alloc(16 * 32, BF16)
        wdt3 = v3(wdt, 16)
        S.dma("pool", lambda e: e.dma_start(out=wdt3, in_=wsrc(win, 2560, 32)), writes=[T("wdt")])
        dtw = [A.alloc(32 * 8) for _ in range(2)]
        acsT_sb = [A.alloc(128) for _ in range(2)]
        for tt in tts:
            w8 = dtw[tt % 2]; wn = "dtw%d" % (tt % 2)
            x1 = w8[:, 0:32]; ex = w8[:, 32:64]; dt_ = w8[:, 64:96]; lndt = w8[:, 96:128]; dA = w8[:, 128:160]; acs = w8[:, 160:192]; dd = w8[:, 192:224]
            pb = 4 + tt % 2; pbn = "psb%d" % pb
            P = bank(pb)
            for kc in range(16):
                S.op("pe", lambda e, kc=kc, tt=tt, P=P: e.matmul(P[:, 0:32], hT3[:, kc, tt * 128:(tt + 1) * 128], wdt3[:, kc, :], start=(kc == 0), stop=(kc == 15)),
                     reads=[T("hTk%d" % kc), T("wdt")], writes=[T(pbn)])
            S.op("dve", lambda e, P=P, x1=x1: e.tensor_tensor(x1, P[:, 0:32], dtb_bc, ALU.add), reads=[T(pbn), T("rowbc")], writes=[T(wn)])
            S.op("act", lambda e, x1=x1, ex=ex: e.activation(ex, x1, AF.Exp), reads=[T(wn)], writes=[T(wn)])
            S.op("act", lambda e, dt_=dt_, ex=ex: e.activation(dt_, ex, AF.Ln, bias=1.0), reads=[T(wn)], writes=[T(wn)])
            S.op("act", lambda e, dt_=dt_, lndt=lndt: e.activation(lndt, dt_, AF.Ln), reads=[T(wn)], writes=[T(wn)])
            S.op("dve", lambda e, dt_=dt_, dA=dA: e.tensor_tensor(dA, dt_, a_bc, ALU.mult), reads=[T(wn), T("a_bc")], writes=[T(wn)])
            S.op("pe", lambda e, P=P, dA=dA: e.matmul(P[:, 32:48], TRIU, dA[:, 0:16], start=True, stop=True), reads=[T(wn), T("cst")], writes=[T(pbn)])
            S.op("pe", lambda e, P=P, dA=dA: e.matmul(P[:, 48:64], TRIL, dA[:, 16:32], start=True, stop=True), reads=[T(wn), T("cst")], writes=[T(pbn)])
            S.op("pe", lambda e, P=P, dA=dA: e.matmul(P[:, 64:96], ONES, dA, start=True, stop=True), reads=[T(wn), T("cst")], writes=[T(pbn)])
            S.op("dve", lambda e, P=P, acs=acs: e.tensor_copy(acs, P[:, 32:64]), reads=[T(pbn)], writes=[T(wn)])
            sl = slice(tt * 32, (tt + 1) * 32)
            S.op("dve", lambda e, lndt=lndt, acs=acs, sl=sl: e.tensor_tensor(biasall[:, sl], lndt, acs, ALU.subtract), reads=[T(wn)], writes=[T("dts")])
            S.op("act", lambda e, acs=acs, sl=sl: e.activation(Eall[:, sl], acs, AF.Exp), reads=[T(wn)], writes=[T("dts")])
            S.op("dve", lambda e, P=P, acs=acs, dd=dd: e.tensor_tensor(dd, P[:, 64:96], acs, ALU.subtract), reads=[T(pbn), T(wn)], writes=[T(wn)])
            S.op("act", lambda e, dd=dd: e.activation(dd, dd, AF.Exp), reads=[T(wn)], writes=[T(wn)])
            S.op("dve", lambda e, dd=dd, dt_=dt_, sl=sl: e.tensor_tensor(wstall[:, sl], dd, dt_, ALU.mult), reads=[T(wn)], writes=[T("dts")])
            S.op("act", lambda e, P=P, sl=sl: e.activation(cdall[:, sl], P[:, 64:96], AF.Exp), reads=[T(pbn)], writes=[T("dts")])
            S.op("pe", lambda e, P=P, acs=acs: e.transpose(P[0:32, 128:256], acs, IDF), reads=[T(wn), T("cst")], writes=[T(pbn)])
            at = acsT_sb[tt % 2]; atn = "acsT%d" % (tt % 2)
            S.op("act", lambda e, P=P, at=at: e.copy(at[0:32, :], P[0:32, 128:256]), reads=[T(pbn)], writes=[T(atn)])
            S.dma("act", lambda e, at=at, tt=tt: e.dma_start(out=acs_d[tt], in_=at[0:32, :]), reads=[T(atn)], writes=[T("acsd%d" % tt)])
        if debug and layer == 0:
            S.dma("sp", lambda e: e.dma_start(out=dbg["dts"], in_=dts), reads=[T("dts")], writes=[T("dbg_dts")])
        A.release(mdtm)
        mx = A.mark()
        raw = A.alloc(PADW); acc = A.alloc(PADW)
        xbcT = [A.alloc(PADW, BF16) for _ in range(2)]
        stg = A.alloc(NTT * 512, BF16)
        stg3 = v3(stg, NTT)
        S.op("dve", lambda e: e.memset(raw, 0.0), writes=[T("raw")])
        cwv = vecs[:, vo + OFF_CW: vo + OFF_CW + 60]
        cbv = vecs[:, vo + OFF_CB: vo + OFF_CB + 12]
        q = 0
        NCV = 2308
        for i in range(6):
            wt, wtile = wload(wsrc(win, 1024 + i * 256, 256), 256)
            conv_step(2)
            for j in range(2):
                blk = i * 2 + j
                for (t0, n) in TGROUPS:
                    pb = q % 4; q += 1
                    for kc in range(16):
                        S.op("pe", lambda e, kc=kc, wt=wt, j=j, pb=pb, t0=t0, n=n: e.matmul(bank(pb)[:, 0:n], wt[:, kc, j * 128:(j + 1) * 128], hT3[:, kc, t0:t0 + n], start=(kc == 0), stop=(kc == 15)),
                             reads=[T("hTk%d" % kc), wtile], writes=[T("psb%d" % pb)])
                    c0 = pcol(t0)
                    S.op("act", lambda e, pb=pb, c0=c0, n=n: e.copy(raw[:, c0:c0 + n], bank(pb)[:, 0:n]), reads=[T("psb%d" % pb)], writes=[T("raw")])
                S.op("dve", lambda e, blk=blk: e.tensor_scalar(acc[:, 2:2 + NCV], raw[:, 0:NCV], cwv[:, blk:blk + 1], None, ALU.mult), reads=[T("raw"), T("vecs")], writes=[T("acc")])
                for k in range(1, 5):
                    S.op("dve", lambda e, blk=blk, k=k: e.scalar_tensor_tensor(acc[:, 2:2 + NCV], raw[:, k:k + NCV], cwv[:, k * 12 + blk:k * 12 + blk + 1], acc[:, 2:2 + NCV], ALU.mult, ALU.add),
                         reads=[T("raw"), T("vecs"), T("acc")], writes=[T("acc")])
                if blk < 10:
                    xb = xbcT[blk % 2]; xbn = "xbcT%d" % (blk % 2)
                elif blk < 12:
                    xb = CT[blk - 10]; xbn = "BCT"
                if blk in (8, 9):
                    xb2 = BT[blk - 8]
                    S.op("act", lambda e, xb2=xb2, blk=blk: e.activation(xb2[:, 2:2 + NCV], acc[:, 2:2 + NCV], AF.Silu, bias=cbv[:, blk:blk + 1]), reads=[T("acc"), T("vecs")], writes=[T("BCT")])
                    xb = xb2; xbn = "BCT"
                else:
                    S.op("act", lambda e, xb=xb, blk=blk: e.activation(xb[:, 2:2 + NCV], acc[:, 2:2 + NCV], AF.Silu, bias=cbv[:, blk:blk + 1]), reads=[T("acc"), T("vecs")], writes=[T(xbn)])
                if blk < 10:
                    jj = blk % 4
                    for tq in range(0, NTT, 4):
                        pb = 4 + (tq // 4) % 2; pbn = "psb%d" % pb
                        Pb = bank(pb).bitcast(BF16)
                        nt = min(4, NTT - tq)
                        for u in range(nt):
                            tt = tq + u
                            c0 = pcol(tt * 128)
                            S.op("pe", lambda e, xb=xb, c0=c0, Pb=Pb, u=u: e.transpose(Pb[:, u * 128:(u + 1) * 128], xb[:, c0:c0 + 128], idb),
                                 reads=[T(xbn), T("idb")], writes=[T(pbn)])
                        S.op("dve", lambda e, Pb=Pb, tq=tq, nt=nt, jj=jj: e.tensor_copy(stg3[:, tq:tq + nt, jj * 128:(jj + 1) * 128], Pb[:, 0:nt * 128].rearrange("p (a b) -> p a b", a=nt)),
                             reads=[T(pbn)], writes=[T("stg")])
                    if blk in (3, 7):
                        h0 = (blk // 4) * 512
                        S.dma("sp", lambda e, h0=h0: e.dma_start(out=xs_d[:, :, h0:h0 + 512].rearrange("t p c -> p t c"), in_=stg3), reads=[T("stg")], writes=[T("xsd")])
                    if blk == 9:
                        S.dma("sp", lambda e: e.dma_start(out=bt_d.rearrange("t p c -> p t c"), in_=stg3[:, :, 0:256]), reads=[T("stg")], writes=[T("btd")])
        if debug and layer == 0:
            S.dma("sp", lambda e: e.dma_start(out=dbg["bct"], in_=BCT), reads=[T("BCT")], writes=[T("dbg_bct")])
        A.release(mx)
        if True:
            msc = A.mark()
            raw = A.alloc(PADW); acc = A.alloc(PADW); ucT = A.alloc(2304)
            yscT = [A.alloc(2304, BF16) for _ in range(2)]
            S.op("dve", lambda e: e.memset(raw, 0.0), writes=[T("raw")])
            scw = vecs[:, vo + OFF_SCW: vo + OFF_SCW + 24]
            q = 0
            for i2 in range(4):
                wgc, tgc = wload(wsrc(win, 3616 + i2 * 256, 256), 256)
                conv_step(2)
                wvl, tvl = wload(wsrc(win, 4640 + i2 * 256, 256), 256)
                conv_step(2)
                wgb, tgb = wload(wsrc(win, 2592 + i2 * 256, 256), 256)
                conv_step(2)
                for j in range(2):
                    blk = i2 * 2 + j
                    for step, (wt, wtile) in enumerate([(wgc, tgc), (wvl, tvl), (wgb, tgb)]):
                        if step == 2:
                            S.op("dve", lambda e, blk=blk: e.tensor_scalar(acc[:, 2:2 + NCV], raw[:, 1:1 + NCV], scw[:, blk:blk + 1], None, ALU.mult), reads=[T("raw"), T("vecs")], writes=[T("acc")])
                            for k in range(1, 3):
                                S.op("dve", lambda e, blk=blk, k=k: e.scalar_tensor_tensor(acc[:, 2:2 + NCV], raw[:, 1 + k:1 + k + NCV], scw[:, k * 8 + blk:k * 8 + blk + 1], acc[:, 2:2 + NCV], ALU.mult, ALU.add),
                                     reads=[T("raw"), T("vecs"), T("acc")], writes=[T("acc")])
                        ys = yscT[blk % 2]; ysn = "yscT%d" % (blk % 2)
                        for (t0, n) in TGROUPS:
                            if last and t0 == 0:
                                continue
                            pb = q % 4; q += 1
                            for kc in range(16):
                                S.op("pe", lambda e, kc=kc, wt=wt, j=j, pb=pb, t0=t0, n=n: e.matmul(bank(pb)[:, 0:n], wt[:, kc, j * 128:(j + 1) * 128], hT3[:, kc, t0:t0 + n], start=(kc == 0), stop=(kc == 15)),
                                     reads=[T("hTk%d" % kc), wtile], writes=[T("psb%d" % pb)])
                            c0 = pcol(t0)
                            if step == 0:
                                S.op("act", lambda e, pb=pb, t0=t0, n=n: e.copy(ucT[:, t0:t0 + n], bank(pb)[:, 0:n]), reads=[T("psb%d" % pb)], writes=[T("ucT")])
                            elif step == 1:
                                S.op("dve", lambda e, pb=pb, t0=t0, n=n, c0=c0: e.tensor_tensor(raw[:, c0:c0 + n], bank(pb)[:, 0:n], ucT[:, t0:t0 + n], ALU.mult),
                                     reads=[T("psb%d" % pb), T("ucT")], writes=[T("raw")])
                            else:
                                S.op("dve", lambda e, pb=pb, t0=t0, n=n, c0=c0, ys=ys: e.tensor_tensor(ys[:, t0:t0 + n], bank(pb)[:, 0:n], acc[:, c0:c0 + n], ALU.mult),
                                     reads=[T("psb%d" % pb), T("acc")], writes=[T(ysn)])
                    lo = 256 if last else 0
                    S.dma("sp", lambda e, ys=ys, blk=blk, lo=lo: e.dma_start(out=ysc_d[blk, :, lo:2304], in_=ys[:, lo:2304]), reads=[T(ysn)], writes=[T("yscd")])
            A.release(msc)
        A.release(mP)
        if stop == "P%d" % layer:
            break

        S.barrier()
        mS = A.mark()
        mixT = A.alloc(8 * 2304, BF16)
        mixT3 = v3(mixT, 8)
        mS2 = A.mark()
        Hs = [A.alloc(1024), A.alloc(1024)]
        hbf = [A.alloc(1024, BF16) for _ in range(2)]
        xs_sb = [A.alloc(1024, BF16) for _ in range(3)]
        bt_sb = [A.alloc(256, BF16) for _ in range(3)]
        szt = [A.alloc(1024, BF16) for _ in range(3)]
        hinb = [A.alloc(1024, BF16) for _ in range(3)]
        Abc = [A.alloc(32 * 128) for _ in range(3)]
        MT = [A.alloc(16 * 128, BF16) for _ in range(2)]
        Xp = A.alloc(1024, BF16)
        t1 = A.alloc(1024); t2 = A.alloc(1024); t3 = A.alloc(1024); ygn = A.alloc(1024); ssb = A.alloc(8)
        gsv = vecs[:, vo + OFF_GS: vo + OFF_GS + 8]

        def h3(ap):
            return ap.rearrange("p (h q) -> p h q", h=16)

        def bc16(ap16):
            return ap16.unsqueeze(2).broadcast_to([128, 16, 64])

        def load_chunk(tt, slot):
            S.dma("sp", lambda e: e.dma_start(out=xs_sb[slot], in_=xs_d[tt]), reads=[T("xsd")], writes=[T("xs_sb%d" % slot)])
            S.dma("sp", lambda e: e.dma_start(out=bt_sb[slot], in_=bt_d[tt]), reads=[T("btd")], writes=[T("bt_sb%d" % slot)])

        def update_H(d, tt, slot):
            Hn = "H%d" % d
            H = Hs[d]
            ws = wstall[:, tt * 32 + d * 16: tt * 32 + d * 16 + 16]
            cd = cdall[:, tt * 32 + d * 16: tt * 32 + d * 16 + 16]
            S.op("dve", lambda e: e.tensor_tensor(h3(Xp), h3(xs_sb[slot]), bc16(ws), ALU.mult), reads=[T("xs_sb%d" % slot), T("dts")], writes=[T("Xp")])
            st = PSW[3]
            for g in range(2):
                S.op("pe", lambda e, g=g: e.matmul(st[:, g * 512:(g + 1) * 512], bt_sb[slot][:, g * 128:(g + 1) * 128], Xp[:, g * 512:(g + 1) * 512], start=True, stop=True),
                     reads=[T("bt_sb%d" % slot), T("Xp")], writes=[T("ps_st")])
            S.op("dve", lambda e: e.tensor_tensor(h3(H), h3(H), bc16(cd), ALU.mult), reads=[T(Hn), T("dts")], writes=[T(Hn)])
            S.op("dve", lambda e: e.tensor_tensor(H, H, st[:, :], ALU.add), reads=[T(Hn), T("ps_st")], writes=[T(Hn)])

        S.op("dve", lambda e: e.memset(Hs[1], 0.0), writes=[T("H1")])
        S.op("dve", lambda e: e.memset(Hs[0], 0.0), writes=[T("H0")])
        orderB = [1, 0] + list(range(NTT - 1, 1, -1))
        for qi, tt in enumerate(orderB):
            slot = qi % 2
            load_chunk(tt, slot)
            if not (last and tt < 2):
                hb = hbf[qi % 2]; hbn = "hbf%d" % (qi % 2)
                S.op("act", lambda e, hb=hb: e.copy(hb, Hs[1]), reads=[T("H1")], writes=[T(hbn)])
                S.dma("act", lambda e, hb=hb, tt=tt: e.dma_start(out=hinb_d[tt], in_=hb), reads=[T(hbn)], writes=[T("hinbd%d" % tt)])
            if tt != 2:
                update_H(1, tt, slot)
        if stop == "SB%d" % layer:
            break
        orderF = list(range(NTT))
        MT4 = [[MT[0], MT[1]], [A.alloc(16 * 128, BF16), A.alloc(16 * 128, BF16)]]
        Xp2 = [Xp, A.alloc(1024, BF16)]
        ygn2 = [ygn, A.alloc(1024)]

        def do_y(tt):
            return not (last and tt < 2)

        def stage_loads(qi):
            tt = orderF[qi]; slot = qi % 2; s3 = qi % 3
            load_chunk(tt, s3)
            if do_y(tt):
                S.dma("sp", lambda e: e.dma_start(out=szt[s3], in_=sz_d[tt]), reads=[T("szd%d" % tt)], writes=[T("szt%d" % s3)])
                S.dma("sp", lambda e: e.dma_start(out=hinb[s3], in_=hinb_d[tt]), reads=[T("hinbd%d" % tt)], writes=[T("hinb%d" % s3)])
                S.dma("sp", lambda e: e.dma_start(out=Abc[s3], in_=acs_d[tt].rearrange("a b -> (a b)").partition_broadcast(128)), reads=[T("acsd%d" % tt)], writes=[T("Abc%d" % s3), T("Abc%d_0" % s3), T("Abc%d_1" % s3)])

        def stage1(qi):
            tt = orderF[qi]; slot = qi % 2; s3 = qi % 3
            c0 = pcol(tt * 128)
            if tt != NTT - 1:
                ws = wstall[:, tt * 32: tt * 32 + 16]
                S.op("dve", lambda e: e.tensor_tensor(h3(Xp2[slot]), h3(xs_sb[s3]), bc16(ws), ALU.mult), reads=[T("xs_sb%d" % s3), T("dts")], writes=[T("Xp%d" % slot)])
            if not do_y(tt):
                return
            cbp = bank(0)[:, slot * 256:(slot + 1) * 256]
            cbn = "ps_cb"
            for g in range(2):
                S.op("pe", lambda e, g=g: e.matmul(cbp[:, g * 128:(g + 1) * 128], BT[g][:, c0:c0 + 128], CT[g][:, c0:c0 + 128], start=True, stop=True),
                     reads=[T("BCT")], writes=[T(cbn)])
            A3 = Abc[s3].rearrange("p (a l) -> p a l", a=32)
            An = "Abc%d" % s3
            for d in range(2):
                Ad = A3[:, d * 16:(d + 1) * 16, :]
                And = An + "_%d" % d
                eng = "pool" if d == 0 else "dve"
                S.op(eng, lambda e, Ad=Ad, d=d: e.tensor_tensor(Ad, Ad, NEGM[d].unsqueeze(1).broadcast_to([128, 16, 128]), ALU.add), reads=[T(An), T("cst")], writes=[T(And)])
                bs = biasall[:, tt * 32 + d * 16: tt * 32 + d * 16 + 16]
                S.op(eng, lambda e, Ad=Ad, bs=bs: e.tensor_tensor(Ad, Ad, bs.unsqueeze(2).broadcast_to([128, 16, 128]), ALU.add), reads=[T(And), T("dts")], writes=[T(And)])
                Me = Ad
                S.op("act", lambda e, Ad=Ad, Me=Me: e.activation(Me, Ad, AF.Exp), reads=[T(And)], writes=[T(And)])
                M3 = v3(MT4[slot][d], 16)
                for g in range(2):
                    S.op("dve", lambda e, g=g, M3=M3, Me=Me: e.tensor_tensor(M3[:, g * 8:(g + 1) * 8, :], Me[:, g * 8:(g + 1) * 8, :],
                                                                        cbp[:, g * 128:(g + 1) * 128].unsqueeze(1).broadcast_to([128, 8, 128]), ALU.mult),
                         reads=[T(And), T(cbn)], writes=[T("MT%d_%d" % (slot, d))])

        def stage3b(qi):
            tt = orderF[qi]; slot = qi % 2; s3 = qi % 3
            yg = ygn2[slot]
            for half in range(2):
                pT = bank(1)
                for u in range(4):
                    kc = half * 4 + u
                    S.op("pe", lambda e, kc=kc, u=u: e.transpose(pT[:, u * 128:(u + 1) * 128], yg[:, kc * 128:(kc + 1) * 128], IDF), reads=[T("ygn%d" % slot), T("cst")], writes=[T("ps_yT")])
                for u in range(4):
                    kc = half * 4 + u
                    S.op("act", lambda e, kc=kc, u=u: e.activation(mixT3[:, kc, tt * 128:(tt + 1) * 128], pT[:, u * 128:(u + 1) * 128], AF.Copy, scale=gsv[:, kc:kc + 1]),
                         reads=[T("ps_yT"), T("vecs")], writes=[T("mixT")])

        def stage2(qi, prev_y):
            tt = orderF[qi]; slot = qi % 2; s3 = qi % 3
            c0 = pcol(tt * 128)
            yps = PSW[1]; yo = PSW[2]; st = PSW[3]
            if do_y(tt):
                for h in range(16):
                    for d in range(2):
                        S.op("pe", lambda e, h=h, d=d: e.matmul(yps[:, h * 64:(h + 1) * 64], v3(MT4[slot][d], 16)[:, h, :], xs_sb[s3][:, h * 64:(h + 1) * 64], start=(d == 0), stop=(d == 1)),
                             reads=[T("MT%d_%d" % (slot, d)), T("xs_sb%d" % s3)], writes=[T("ps_y")])
            if prev_y is not None:
                stage3b(prev_y)
            if do_y(tt):
                hf = hbf[slot]; hfn = "hbf%d" % slot
                S.op("act", lambda e: e.copy(hf, Hs[0]), reads=[T("H0")], writes=[T(hfn)])
                for d in range(2):
                    hin = hf if d == 0 else hinb[s3]
                    hinn = hfn if d == 0 else "hinb%d" % s3
                    for g in range(2):
                        S.op("pe", lambda e, g=g, hin=hin: e.matmul(yo[:, g * 512:(g + 1) * 512], CT[g][:, c0:c0 + 128], hin[:, g * 512:(g + 1) * 512], start=True, stop=True),
                             reads=[T("BCT"), T(hinn)], writes=[T("ps_yo")])
                    Ev = Eall[:, tt * 32 + d * 16: tt * 32 + d * 16 + 16]
                    dst = t1 if d == 0 else t2
                    S.op("dve", lambda e, dst=dst, Ev=Ev: e.tensor_tensor(h3(dst), h3(yo[:, :]), bc16(Ev), ALU.mult), reads=[T("ps_yo"), T("dts")], writes=[T("t1" if d == 0 else "t2")])
            if tt != NTT - 1:
                cd = cdall[:, tt * 32: tt * 32 + 16]
                for g in range(2):
                    S.op("pe", lambda e, g=g: e.matmul(st[:, g * 512:(g + 1) * 512], bt_sb[s3][:, g * 128:(g + 1) * 128], Xp2[slot][:, g * 512:(g + 1) * 512], start=True, stop=True),
                         reads=[T("bt_sb%d" % s3), T("Xp%d" % slot)], writes=[T("ps_st")])
                S.op("dve", lambda e: e.tensor_tensor(h3(Hs[0]), h3(Hs[0]), bc16(cd), ALU.mult), reads=[T("H0"), T("dts")], writes=[T("H0")])
                S.op("dve", lambda e: e.tensor_tensor(Hs[0], Hs[0], st[:, :], ALU.add), reads=[T("H0"), T("ps_st")], writes=[T("H0")])

        def stage3a(qi):
            tt = orderF[qi]; slot = qi % 2; s3 = qi % 3
            yps = PSW[1]
            S.op("pool", lambda e: e.tensor_tensor(t1, t1, t2, ALU.add), reads=[T("t1"), T("t2")], writes=[T("t1")])
            S.op("pool", lambda e: e.tensor_tensor(h3(t3), h3(xs_sb[s3]), bc16(dsk_bc), ALU.mult), reads=[T("xs_sb%d" % s3), T("rowbc")], writes=[T("t3")])
            S.op("pool", lambda e: e.tensor_tensor(t1, t1, t3, ALU.add), reads=[T("t1"), T("t3")], writes=[T("t1")])
            S.op("dve", lambda e: e.tensor_tensor(t1, t1, yps[:, :], ALU.add), reads=[T("t1"), T("ps_y")], writes=[T("t1")])
            S.op("dve", lambda e: e.tensor_tensor(t1, t1, szt[s3], ALU.mult), reads=[T("t1"), T("szt%d" % s3)], writes=[T("t1")])
            k = uid("s")
            ssq = ssb[:, (slot * 2):(slot * 2) + 1]; rstd = ssb[:, (slot * 2) + 1:(slot * 2) + 2]
            S.op("act", lambda e: e.activation(t2, t1, AF.Square, accum_out=ssq), reads=[T("t1")], writes=[T("t2"), T(k + "ss")])
            rms_rstd("dve", ssq, rstd, 1024.0, k + "ss", k + "rs")
            S.op("act", lambda e: e.activation(ygn2[slot], t1, AF.Copy, scale=rstd), reads=[T("t1"), T(k + "rs")], writes=[T("ygn%d" % slot)])

        NQ = len(orderF)
        stage_loads(0); stage1(0)
        prev_y = None
        for qi in range(NQ):
            if qi + 1 < NQ:
                stage_loads(qi + 1); stage1(qi + 1)
            stage2(qi, prev_y)
            prev_y = None
            if do_y(orderF[qi]):
                stage3a(qi)
                prev_y = qi
        if prev_y is not None:
            stage3b(prev_y)
        if debug and layer == 0:
            S.dma("sp", lambda e: e.dma_start(out=dbg["mixT"], in_=mixT), reads=[T("mixT")], writes=[T("dbg_mixT")])
        if stop == "S%d" % layer:
            break

        S.barrier()
        A.release(mS2)
        mO = A.mark()
        wo = A.alloc(16 * D, BF16)
        wo3 = v3(wo, 16)
        for i in range(4):
            S.dma("pool", lambda e, i=i: e.dma_start(out=wo3[:, :, i * 512:(i + 1) * 512], in_=wsrc(w_out[layer], i * 512, 512)), writes=[T("wo%d" % i)])
        ysc_sb = [A.alloc(8 * 128, BF16) for _ in range(2)]
        xold = [A.alloc(D) for _ in range(2)]
        tmpo = A.alloc(D)
        g1bc = [A.alloc(D), A.alloc(D) if not last else None]
        for m in range(2):
            if g1bc[m] is not None:
                S.dma("sp", lambda e, m=m: e.dma_start(out=g1bc[m], in_=g1_d[m]), reads=[T("g1d%d" % m)], writes=[T("g1bc")])
        ttsO = [tt for tt in range(NTT) if not (last and tt < 2)]
        def o_loads(qi):
            tt = ttsO[qi]; slot = qi % 2
            ys3 = v3(ysc_sb[slot], 8)
            S.dma("sp", lambda e: e.dma_start(out=ys3, in_=ysc_d[:, :, tt * 128:(tt + 1) * 128].rearrange("b p c -> p b c")), reads=[T("yscd")], writes=[T("ysc_sb%d" % slot)])
            rows, toks = tile_rows(layer, tt, layer == 0)
            load_tile(rows, toks, xold[slot], "xold%d" % slot)

        o_loads(0)
        for qi, tt in enumerate(ttsO):
            slot = qi % 2
            ys3 = v3(ysc_sb[slot], 8)
            if qi + 1 < len(ttsO):
                o_loads(qi + 1)
            conv_step(1)
            for nb in range(4):
                pbk = (qi % 2) * 4 + nb
                for kc in range(16):
                    lhs = mixT3[:, kc, tt * 128:(tt + 1) * 128] if kc < 8 else ys3[:, kc - 8, :]
                    S.op("pe", lambda e, lhs=lhs, kc=kc, nb=nb, pbk=pbk: e.matmul(bank(pbk), lhs, wo3[:, kc, nb * 512:(nb + 1) * 512], start=(kc == 0), stop=(kc == 15)),
                         reads=[T("mixT"), T("ysc_sb%d" % slot), T("wo%d" % nb)], writes=[T("psb%d" % pbk)])
            gb = g1bc[1 if tt < 2 else 0]
            for nb in range(4):
                pbk = (qi % 2) * 4 + nb
                sl = slice(nb * 512, (nb + 1) * 512)
                S.op("dve", lambda e, pbk=pbk, sl=sl, gb=gb: e.tensor_tensor(tmpo[:, sl], bank(pbk), gb[:, sl], ALU.mult), reads=[T("psb%d" % pbk), T("g1bc")], writes=[T("tmpo")])
            S.op("pool", lambda e, slot=slot: e.tensor_tensor(xold[slot], xold[slot], tmpo, ALU.add), reads=[T("xold%d" % slot), T("tmpo")], writes=[T("xold%d" % slot)])
            drows, dtoks = tile_rows(layer, tt, False)
            store_tile(drows, dtoks, xold[slot], "xold%d" % slot)
        conv_step(len(conv_pending))
        A.release(mO)
        A.release(mS)
        A.release(m_mix)
        if stop == "O%d" % layer:
            break

        tt_nat = list(range(NTT)) if not last else list(range(2, NTT))
        ftiles = [tt_nat[i:i + 6] for i in range(0, len(tt_nat), 6)]
        for fi, ft in enumerate(ftiles):
            ntt = len(ft)
            Tn = ntt * 128
            S.barrier()
            mF = A.mark()
            actT = A.alloc(44 * Tn, BF16)
            actT3 = v3(actT, 44)
            mF2 = A.mark()
            h2T = A.alloc(16 * Tn, BF16)
            h2T3 = v3(h2T, 16)
            NW = 4
            wring = [A.alloc(16 * 512, BF16) for _ in range(NW)]
            wstate["i"] = 0
            sgb = A.alloc(1024)
            sgs = [sgb[:, 0:512], sgb[:, 512:1024]]
            xts = [A.alloc(D), A.alloc(D)]
            xnbs = [A.alloc(D, BF16), A.alloc(D, BF16)]
            junk = sgb.bitcast(BF16); ssbs = [A.alloc(8), A.alloc(8)]

            def f1_load(si):
                tt = ft[si]
                S.dma("sp", lambda e: e.dma_start(out=xts[si % 2], in_=nat_rows(tt)), reads=[T("xr%d" % tt)], writes=[T("xt%d" % (si % 2))])

            def f1_A(si):
                norm_A(xts[si % 2], "xt%d" % (si % 2), junk, xnbs[si % 2], "xnb%d" % (si % 2), ssbs[si % 2])

            def f1_B(si):
                isctx = ft[si] < 2
                norm_B(xnbs[si % 2], "xnb%d" % (si % 2), ABs(6 if isctx else 4), ABs(7 if isctx else 5), h2T3, si * 128, "h2T", (0, 1))

            f1_load(0)
            if ntt > 1:
                f1_load(1)
            f1_A(0)
            for si in range(ntt):
                if si + 1 < ntt:
                    f1_A(si + 1)
                if si + 2 < ntt:
                    f1_load(si + 2)
                f1_B(si)
            subs = [(t0, min(512, Tn - t0)) for t0 in range(0, Tn, 512)]
            q = 0
            for j2 in range(11):
                wg, tg_ = wreload(j2, 512)
                wu, tu_ = wreload(11 + j2, 512)
                for j in range(4):
                    jb = j2 * 4 + j
                    for (t0, n) in subs:
                        pg = (q % 2) * 2; pu = pg + 1; q += 1
                        for kc in range(16):
                            S.op("pe", lambda e, kc=kc, wg=wg, j=j, pg=pg, t0=t0, n=n: e.matmul(bank(pg)[:, 0:n], wg[:, kc, j * 128:(j + 1) * 128], h2T3[:, kc, t0:t0 + n], start=(kc == 0), stop=(kc == 15)),
                                 reads=[T("h2Tk%d" % kc), tg_], writes=[T("psb%d" % pg)])
                        for kc in range(16):
                            S.op("pe", lambda e, kc=kc, wu=wu, j=j, pu=pu, t0=t0, n=n: e.matmul(bank(pu)[:, 0:n], wu[:, kc, j * 128:(j + 1) * 128], h2T3[:, kc, t0:t0 + n], start=(kc == 0), stop=(kc == 15)),
                                 reads=[T("h2Tk%d" % kc), tu_], writes=[T("psb%d" % pu)])
                        sg = sgs[q % 2]; sgn = "sg%d" % (q % 2)
                        S.op("act", lambda e, sg=sg, pg=pg, n=n: e.activation(sg[:, 0:n], bank(pg)[:, 0:n], AF.Silu), reads=[T("psb%d" % pg)], writes=[T(sgn)])
                        S.op("dve", lambda e, sg=sg, pu=pu, n=n, jb=jb, t0=t0: e.tensor_tensor(actT3[:, jb, t0:t0 + n], sg[:, 0:n], bank(pu)[:, 0:n], ALU.mult),
                             reads=[T(sgn), T("psb%d" % pu)], writes=[T("actT")])
                if layer == 0 and ada_pending:
                    ada_ctr[0] += 1
                    if ada_ctr[0] % 3 != 0:
                        bi_, blk_, i_ = ada_pending.pop(0)
                        ada_step(1, bi_, blk_, i_, bank(4), "ps_mp1")
            S.barrier()
            A.release(mF2)
            wd = [A.alloc(44 * 256, BF16) for _ in range(2)]
            xstage = A.alloc(ntt * D)
            xst3 = v3(xstage, ntt)
            oTs = [A.alloc(512) for _ in range(2)]
            fngbc = None
            if last:
                fngbc = A.alloc(D)
                S.dma("sp", lambda e: e.dma_start(out=fngbc, in_=fng_in[0].partition_broadcast(128)), writes=[T("fngbc")])
                ssb = A.alloc(8)
                junk = A.alloc(D)
            for si, tt in enumerate(ft):
                S.dma("sp", lambda e, si=si, tt=tt: e.dma_start(out=xst3[:, si, :], in_=nat_rows(tt)), reads=[T("xr%d" % tt)], writes=[T("xst%d" % si)])
            q = 0
            wdsrc = w_down[layer].rearrange("(k p) n -> p k n", p=128)

            def f3_tail(g):
                (ot, otn, pT, t0, n, nb) = g
                for u in range(n // 128):
                    S.op("pe", lambda e, u=u: e.transpose(bank(pT)[:, u * 128:(u + 1) * 128], ot[:, u * 128:(u + 1) * 128], IDF), reads=[T(otn), T("cst")], writes=[T("psb%d" % pT)])
                for u in range(n // 128):
                    si = (t0 // 128) + u
                    S.op("dve", lambda e, si=si, u=u: e.tensor_tensor(xst3[:, si, nb * 128:(nb + 1) * 128], xst3[:, si, nb * 128:(nb + 1) * 128], bank(pT)[:, u * 128:(u + 1) * 128], ALU.add),
                         reads=[T("xst%d" % si), T("psb%d" % pT)], writes=[T("xst%d" % si)])

            prev_g = None
            for n8 in range(8):
                wdt_ = wd[n8 % 2]; wdn = "wd%d" % (n8 % 2)
                wd3 = v3(wdt_, 44)
                S.dma("sp", lambda e: e.dma_start(out=wdt_, in_=wds_d[n8]), reads=[T("wdsd%d_%d" % (n8, kq_)) for kq_ in range(4)], writes=[T(wdn)])
                for nbi in range(2):
                    nb = n8 * 2 + nbi
                    for (t0, n) in subs:
                        pb = q % 2; q += 1
                        for jb in range(44):
                            S.op("pe", lambda e, jb=jb: e.matmul(bank(pb)[:, 0:n], wd3[:, jb, nbi * 128:(nbi + 1) * 128], actT3[:, jb, t0:t0 + n], start=(jb == 0), stop=(jb == 43)),
                                 reads=[T("actT"), T(wdn)], writes=[T("psb%d" % pb)])
                        ot = oTs[q % 2]; otn = "oTs%d" % (q % 2)
                        for u in range(n // 128):
                            tt = ft[(t0 // 128) + u]
                            m = 1 if tt < 2 else 0
                            S.op("act", lambda e, u=u, m=m: e.activation(ot[:, u * 128:(u + 1) * 128], bank(pb)[:, u * 128:(u + 1) * 128], AF.Copy, scale=G2[:, m * 16 + nb: m * 16 + nb + 1]),
                                 reads=[T("psb%d" % pb), T("modp")], writes=[T(otn)])
                        if prev_g is not None:
                            f3_tail(prev_g)
                        prev_g = (ot, otn, 2 + (q % 2), t0, n, nb)
            f3_tail(prev_g)
            for si, tt in enumerate(ft):
                if not last:
                    S.dma("sp", lambda e, si=si, tt=tt: e.dma_start(out=nat_rows(tt), in_=xst3[:, si, :]), reads=[T("xst%d" % si)], writes=[T("xr%d" % tt)])
                else:
                    k = uid("f")
                    ss = ssb[:, 0:1]; rstd = ssb[:, 1:2]
                    xs_ = xst3[:, si, :]
                    S.op("act", lambda e, xs_=xs_: e.activation(junk, xs_, AF.Square, accum_out=ss), reads=[T("xst%d" % si)], writes=[T("junk"), T(k + "ss")])
                    rms_rstd("dve", ss, rstd, float(D), k + "ss", k + "rs")
                    S.op("dve", lambda e, xs_=xs_: e.scalar_tensor_tensor(xs_, xs_, rstd, fngbc, ALU.mult, ALU.mult), reads=[T("xst%d" % si), T(k + "rs"), T("fngbc")], writes=[T("xst%d" % si)])
                    r0 = (tt - 2) * 128
                    S.dma("sp", lambda e, si=si, r0=r0: e.dma_start(out=out[r0:r0 + 128, :], in_=xst3[:, si, :]), reads=[T("xst%d" % si)], writes=[T("out")])
            A.release(mF)
        A.release(m_layer)
        if stop == "F%d" % layer:
            break

    S.finalize()
    es.close()
    return nc, S, A


def _host_inputs(inp):
    f = np.float32
    cst = np.zeros((128, 768), f)
    s = np.arange(128)[:, None]; l = np.arange(128)[None, :]
    cst[:, 0:128] = np.eye(128, dtype=f)
    cst[:, 128:256] = (s <= l)
    cst[:, 256:384] = (s >= l)
    cst[:, 384:512] = np.where(s <= l, 0.0, NEG)
    cst[:, 512:640] = np.where(s >= l, 0.0, NEG)
    cst[:, 640:768] = 1.0
    idb = np.eye(128).astype(ml_dtypes.bfloat16)

    def pl(v):
        return np.ascontiguousarray(np.asarray(v, f).reshape(-1, 128).T)

    vecs = np.zeros((128, 2 * NV), f)
    rowv = np.zeros((1, 160), f)
    for l_ in range(2):
        o = l_ * NV
        vecs[:, o + OFF_ADAB:o + OFF_ADAB + 96] = pl(inp["ada_b"][l_])
        vecs[:, o + OFF_G1:o + OFF_G1 + 16] = pl(inp["mix_norm_g"][l_])
        vecs[:, o + OFF_G2:o + OFF_G2 + 16] = pl(inp["ffn_norm_g"][l_])
        vecs[:, o + OFF_GS:o + OFF_GS + 8] = pl(inp["ssd_norm_g"][l_])
        for k in range(5):
            vecs[:, o + OFF_CW + k * 12:o + OFF_CW + (k + 1) * 12] = pl(inp["ssd_conv_w"][l_, k])
        vecs[:, o + OFF_CB:o + OFF_CB + 12] = pl(inp["ssd_conv_b"][l_])
        for k in range(3):
            vecs[:, o + OFF_SCW + k * 8:o + OFF_SCW + (k + 1) * 8] = pl(inp["sc_conv_w"][l_, k])
        rowv[0, l_ * 80:l_ * 80 + 32] = np.asarray(inp["ssd_dt_bias"][l_], f).reshape(32)
        rowv[0, l_ * 80 + 32:l_ * 80 + 64] = np.asarray(inp["ssd_a_log"][l_], f).reshape(32)
        rowv[0, l_ * 80 + 64:l_ * 80 + 80] = np.asarray(inp["ssd_d"][l_], f)
    shared = {
        "ada_w": np.ascontiguousarray(inp["ada_w"], f), "ada_b": np.ascontiguousarray(inp["ada_b"], f),
        "w_in": np.ascontiguousarray(inp["w_in"], f), "w_out": np.ascontiguousarray(inp["w_out"], f),
        "w_gate": np.ascontiguousarray(inp["w_gate"], f), "w_up": np.ascontiguousarray(inp["w_up"], f),
        "w_down": np.ascontiguousarray(inp["w_down"], f),
        "vecs": vecs, "rowv": rowv, "fng": np.asarray(inp["final_norm_g"], f).reshape(1, D),
        "cst": cst, "idb": idb,
    }
    maps = []
    cc = np.asarray(inp["c_ctx"], f)
    for b in range(8):
        cvb = np.zeros((128, 16, 2), f)
        cvb[:, :, 0] = pl(inp["c"][b])
        cvb[:, :, 1] = pl(cc)
        m = dict(shared)
        m["x"] = np.ascontiguousarray(inp["x"][b], f)
        m["ctx"] = np.ascontiguousarray(inp["ctx"][b], f)
        m["cv"] = cvb.reshape(128, 32)
        maps.append(m)
    return maps


def kernel(**inputs):
    nc, S, A = build_program()
    maps = _host_inputs(inputs)
    res = run_bass_kernel_spmd(nc, maps, core_ids=list(range(8)))
    return np.stack([np.asarray(r["out"], np.float32) for r in res.results], axis=0)
```
